# Optimizing a Trainium2 kernel written in Bass

```python
import math
import jax, jax.numpy as jnp
from jax import lax
import numpy as np

D_MODEL = 1024
BATCH = 8
SEQ = 2048
DEPTH = 2

N_META = 16
ROPE_THETA = 500000.0
LN_EPS = 1e-5
GLA_HEADS = 4
GLA_DK = 64
GLA_DV = 128
GLA_RANK = 16
GLA_TAU = 16.0
GLA_CHUNK = 64
GLA_W = GLA_HEADS * GLA_DV
DSA_HEADS = 8
DSA_KV_HEADS = 2
DSA_GROUP = DSA_HEADS // DSA_KV_HEADS
DSA_HD = 64
DSA_W = DSA_HEADS * DSA_HD
IDX_HEADS = 8
IDX_HD = 64
TOPK_MAX = 256
Q_BLOCK = 128
ROPE_DIM = DSA_HD // 4
MIX_W = GLA_W + DSA_W
IN_SPLITS = (
    GLA_HEADS * GLA_DK,
    GLA_HEADS * GLA_DK,
    GLA_W,
    GLA_RANK,
    GLA_W,
    DSA_HEADS * DSA_HD,
    DSA_KV_HEADS * DSA_HD,
    DSA_KV_HEADS * DSA_HD,
    IDX_HEADS * IDX_HD,
    IDX_HD,
    IDX_HEADS,
    DSA_W,
)
IN_W = sum(IN_SPLITS)
DEEPNORM_ALPHA = (2.0 * DEPTH) ** 0.25
DEEPNORM_BETA = (8.0 * DEPTH) ** -0.25

kernel_name = "hymba_gla_dsa_deepnorm_trunk"


def _layer_norm(x, g, b):
    xf = x.astype(jnp.float32)
    mu = xf.mean(-1, keepdims=True)
    var = jnp.square(xf - mu).mean(-1, keepdims=True)
    return ((xf - mu) * lax.rsqrt(var + LN_EPS) * g.astype(jnp.float32) + b.astype(jnp.float32)).astype(x.dtype)


def _split_cols(p):
    out, o = [], 0
    for w in IN_SPLITS:
        out.append(p[..., o:o + w])
        o += w
    return out


def _rope_tables(n_pos):
    inv = ROPE_THETA ** (-jnp.arange(0, ROPE_DIM, 2, dtype=jnp.float32) / ROPE_DIM)
    ang = jnp.arange(n_pos, dtype=jnp.float32)[:, None] * inv[None, :]
    return jnp.cos(ang), jnp.sin(ang)


def _partial_rope(x, cos, sin):
    half = ROPE_DIM // 2
    shape = (1, cos.shape[0]) + (1,) * (x.ndim - 3) + (half,)
    c = cos.reshape(shape).astype(x.dtype)
    s = sin.reshape(shape).astype(x.dtype)
    x1, x2, rest = x[..., :half], x[..., half:ROPE_DIM], x[..., ROPE_DIM:]
    return jnp.concatenate([x1 * c - x2 * s, x2 * c + x1 * s, rest], axis=-1)


def _gla(q, k, v, g):
    B, L, H, _ = q.shape
    pad = (-N_META) % GLA_CHUNK
    padw = ((0, 0), (pad, 0), (0, 0), (0, 0))
    q, k, v, g = [jnp.pad(a.astype(jnp.float32), padw) for a in (q, k, v, g)]
    n = (L + pad) // GLA_CHUNK

    def to_chunks(a):
        return a.reshape(B, n, GLA_CHUNK, H, a.shape[-1]).transpose(1, 0, 3, 2, 4)

    causal = jnp.tril(jnp.ones((GLA_CHUNK, GLA_CHUNK), dtype=bool))[:, :, None]

    def step(S, inp):
        qi, ki, vi, gi = inp
        b = jnp.cumsum(gi, axis=2)
        o_inter = jnp.einsum('bhtd,bhde->bhte', qi * jnp.exp(b), S)
        diff = b[:, :, :, None, :] - b[:, :, None, :, :]
        decay = jnp.exp(jnp.where(causal, diff, -jnp.inf))
        A = jnp.einsum('bhtd,bhsd,bhtsd->bhts', qi, ki, decay)
        o = o_inter + jnp.einsum('bhts,bhse->bhte', A, vi)
        b_last = b[:, :, -1:, :]
        S_new = jnp.exp(b_last[:, :, 0, :])[..., None] * S + jnp.einsum(
            'bhsd,bhse->bhde', ki * jnp.exp(b_last - b), vi)
        return S_new, o

    S0 = jnp.zeros((B, H, GLA_DK, GLA_DV), jnp.float32)
    _, o = lax.scan(step, S0, (to_chunks(q), to_chunks(k), to_chunks(v), to_chunks(g)))
    o = o.transpose(1, 0, 3, 2, 4).reshape(B, n * GLA_CHUNK, H, GLA_DV)
    return o[:, pad:]


def _dsa(q, k, v, iq, ik, iw, n_real):
    B, L = q.shape[:2]
    topk = min(TOPK_MAX, n_real // 4)
    nb = -(-L // Q_BLOCK)
    Lp = nb * Q_BLOCK

    def blocks(a):
        a = jnp.pad(a, ((0, 0), (0, Lp - L)) + ((0, 0),) * (a.ndim - 2))
        return a.reshape((B, nb, Q_BLOCK) + a.shape[2:]).swapaxes(0, 1)

    kv = jnp.concatenate([k, v], axis=-1)
    key_pos = jnp.arange(L)

    def one_block(inp):
        start, qb, iqb, iwb = inp
        t = start + jnp.arange(Q_BLOCK)
        dots = jnp.einsum('bqjd,bsd->bqjs', iqb, ik).astype(jnp.float32) * (IDX_HD ** -0.5)
        score = jnp.einsum('bqj,bqjs->bqs', iwb.astype(jnp.float32) * (IDX_HEADS ** -0.5), jax.nn.relu(dots))
        causal = key_pos[None, :] <= t[:, None]
        score = jnp.where(causal[None], score, -jnp.inf)
        _, sel = lax.top_k(score, topk)
        kv_sel = jax.vmap(lambda a, i: a[i])(kv, sel)
        k_sel, v_sel = kv_sel[..., :DSA_HD], kv_sel[..., DSA_HD:]
        valid = sel <= t[None, :, None]
        qg = qb.reshape(B, Q_BLOCK, DSA_KV_HEADS, DSA_GROUP, DSA_HD)
        s = jnp.einsum('bqgrd,bqkgd->bqgrk', qg, k_sel).astype(jnp.float32) * (DSA_HD ** -0.5)
        s = jnp.where(valid[:, :, None, None, :], s, -jnp.inf)
        p = jax.nn.softmax(s, axis=-1).astype(v_sel.dtype)
        o = jnp.einsum('bqgrk,bqkgd->bqgrd', p, v_sel)
        return o.reshape(B, Q_BLOCK, DSA_W)

    starts = jnp.arange(nb) * Q_BLOCK
    out = lax.map(one_block, (starts, blocks(q), blocks(iq), blocks(iw)))
    return out.swapaxes(0, 1).reshape(B, Lp, DSA_W)[:, :L]


def _layer(h, cos, sin, n_real, w_in, gla_wg2, gla_bg, gla_norm_g, idx_k_g, idx_k_b, w_out, ln_g, ln_b):
    B, L, _ = h.shape
    proj = jnp.einsum('bld,de->ble', h, w_in)
    g_q, g_k, g_v, g_lr, g_z, a_q, a_k, a_v, i_q, i_k, i_w, a_z = _split_cols(proj)

    gq = g_q.reshape(B, L, GLA_HEADS, GLA_DK) * (GLA_DK ** -0.5)
    gk = g_k.reshape(B, L, GLA_HEADS, GLA_DK)
    gv = g_v.reshape(B, L, GLA_HEADS, GLA_DV)
    glog = jax.nn.log_sigmoid(jnp.einsum('blr,rk->blk', g_lr, gla_wg2).astype(jnp.float32)
                              + gla_bg.astype(jnp.float32)) / GLA_TAU
    glog = glog.reshape(B, L, GLA_HEADS, GLA_DK)
    o_gla = _gla(gq, gk, gv, glog)
    o_gla = o_gla * lax.rsqrt(jnp.mean(jnp.square(o_gla), -1, keepdims=True) + LN_EPS) * gla_norm_g.astype(jnp.float32)
    o_gla = o_gla.reshape(B, L, GLA_W).astype(h.dtype) * jax.nn.silu(g_z)

    aq = _partial_rope(a_q.reshape(B, L, DSA_HEADS, DSA_HD), cos, sin)
    ak = _partial_rope(a_k.reshape(B, L, DSA_KV_HEADS, DSA_HD), cos, sin)
    av = a_v.reshape(B, L, DSA_KV_HEADS, DSA_HD)
    iq = _partial_rope(i_q.reshape(B, L, IDX_HEADS, IDX_HD), cos, sin)
    ik = _partial_rope(_layer_norm(i_k, idx_k_g, idx_k_b), cos, sin)
    o_dsa = _dsa(aq, ak, av, iq, ik, i_w, n_real) * jax.nn.silu(a_z)

    y = jnp.einsum('blm,md->bld', jnp.concatenate([o_gla, o_dsa], axis=-1), w_out)
    return _layer_norm(DEEPNORM_ALPHA * h + y, ln_g, ln_b)


def setup_inputs(seed: int = 0) -> dict:
    key = jax.random.key(seed)
    ks = jax.random.split(key, 13)
    f32 = jnp.float32
    nrm = lambda k, s: jax.random.normal(k, s, f32)
    return {
        "x": nrm(ks[0], (BATCH, SEQ, D_MODEL)),
        "meta_tokens": nrm(ks[1], (N_META, D_MODEL)),
        "ln_in_g": 1.0 + 0.02 * nrm(ks[2], (D_MODEL,)),
        "ln_in_b": 0.02 * nrm(ks[3], (D_MODEL,)),
        "w_in": nrm(ks[4], (DEPTH, D_MODEL, IN_W)) * D_MODEL ** -0.5,
        "gla_wg2": nrm(ks[5], (DEPTH, GLA_RANK, GLA_HEADS * GLA_DK)) * GLA_RANK ** -0.5,
        "gla_bg": 0.1 * nrm(ks[6], (DEPTH, GLA_HEADS * GLA_DK)),
        "gla_norm_g": 1.0 + 0.02 * nrm(ks[7], (DEPTH, GLA_DV)),
        "idx_k_g": 1.0 + 0.02 * nrm(ks[8], (DEPTH, IDX_HD)),
        "idx_k_b": 0.02 * nrm(ks[9], (DEPTH, IDX_HD)),
        "w_out": nrm(ks[10], (DEPTH, MIX_W, D_MODEL)) * (MIX_W ** -0.5) * DEEPNORM_BETA,
        "ln_g": 1.0 + 0.02 * nrm(ks[11], (DEPTH, D_MODEL)),
        "ln_b": 0.02 * nrm(ks[12], (DEPTH, D_MODEL)),
    }


def reference(x, meta_tokens, ln_in_g, ln_in_b, w_in, gla_wg2, gla_bg, gla_norm_g, idx_k_g, idx_k_b, w_out, ln_g, ln_b):
    B, S, _ = x.shape
    meta = jnp.broadcast_to(meta_tokens[None].astype(x.dtype), (B, N_META, D_MODEL))
    h = jnp.concatenate([meta, x], axis=1)
    h = _layer_norm(h, ln_in_g, ln_in_b)
    cos, sin = _rope_tables(N_META + S)
    for i in range(DEPTH):
        h = _layer(h, cos, sin, S, w_in[i], gla_wg2[i], gla_bg[i], gla_norm_g[i],
                   idx_k_g[i], idx_k_b[i], w_out[i], ln_g[i], ln_b[i])
    return h[:, N_META:]
```

```python
import math
from contextlib import ExitStack

import numpy as np
import concourse.bass as bass
import concourse.mybir as mybir
from concourse.bass_utils import run_bass_kernel_spmd

F32 = mybir.dt.float32
BF16 = mybir.dt.bfloat16
AF = mybir.ActivationFunctionType
ALU = mybir.AluOpType
AX = mybir.AxisListType

T = 2176
NT = 17
NPAD = 112
D = 1024
DEPTH = 2
IN_W = 3416
C_GQ, C_GK, C_GV, C_LR, C_GZ, C_AQ, C_AK, C_AV, C_IQ, C_IK, C_IW, C_AZ = (
    0, 256, 512, 1024, 1040, 1552, 2064, 2192, 2320, 2832, 2896, 2904)
ALPHA = (2.0 * DEPTH) ** 0.25
LN_EPS = 1e-5
NEG = -1.0e30
NEG2 = -3.0e38
TOPK = 256
NBIS = 24
R_FRONT = 4


class Sched:
    EPOCH = 30000

    def __init__(self, nc, es):
        self.nc, self.es = nc, es
        self.engs = {'pe': nc.tensor, 'act': nc.scalar, 'dve': nc.vector, 'pool': nc.gpsimd, 'sp': nc.sync}
        self.prog = {e: [] for e in self.engs}
        self.cnt = {e: 0 for e in self.engs}
        self.sems = {}
        self.semval = {}
        self.waited = {e: {} for e in self.engs}
        self.last_w = {}
        self.readers = {}
        self.pe_rt = {}

    def sem(self, key):
        if key not in self.sems:
            self.sems[key] = self.es.enter_context(self.nc.semaphore("s%d" % len(self.sems)))
        return self.sems[key]

    def _wait(self, e, tok):
        k, v = tok
        if k[0] == 'dma':
            v = self.semval[k]
        if self.waited[e].get(k, 0) >= v:
            return
        if e == 'pe' and k[0] == 'eng' and k[1] == 'pe':
            return
        self.waited[e][k] = v
        self.prog[e].append(('wait', k, v))

    def op(self, e, fn, reads=(), writes=(), dma=None, rt='full'):
        deps = []
        if e == 'pe':
            for b in writes:
                last = self.pe_rt.get(b)
                if last is not None and last[0] != rt:
                    k, v = last[1]
                    if self.waited[e].get(k, 0) < v:
                        self.waited[e][k] = v
                        self.prog[e].append(('wait', k, v))
        for b in reads:
            if b in self.last_w:
                deps.append(self.last_w[b])
        for b in writes:
            if b in self.last_w:
                deps.append(self.last_w[b])
            deps.extend(self.readers.get(b, {}).items())
        for tok in deps:
            self._wait(e, tok)
        fns = fn if isinstance(fn, (list, tuple)) else [fn]
        if dma is None:
            assert len(fns) == 1
            self.cnt[e] += 1
            n = self.cnt[e]
            k = ('eng', e)
            v = n
            inc = 1
        else:
            k = ('dma', dma)
            self.semval[k] = self.semval.get(k, 0) + 16 * len(fns)
            v = self.semval[k]
            inc = 16
        self.sem(k)
        for f in fns:
            self.prog[e].append(('op', f, k, inc, v))
        tok = (k, v)
        if e == 'pe':
            for b in writes:
                self.pe_rt[b] = (rt, tok)
        for b in writes:
            self.last_w[b] = tok
            self.readers[b] = {}
        for b in reads:
            r = self.readers.setdefault(b, {})
            r[k] = max(r.get(k, 0), v)
        return tok

    def wait_tokens(self, e, toks):
        for t in toks:
            self._wait(e, t)

    def emit(self):
        needed = set()
        for e in self.prog:
            for it in self.prog[e]:
                if it[0] == 'wait' and it[1][0] == 'eng':
                    needed.add((it[1], it[2]))
        rank = {}
        for e in self.prog:
            r = 0
            for it in self.prog[e]:
                if it[0] == 'op' and it[2][0] == 'eng':
                    tok = (it[2], it[4])
                    if tok in needed:
                        r += 1
                        rank[tok] = r
        self.n_inc = {e: 0 for e in self.prog}
        with self.nc.Block() as block:
            def mk(e):
                def f(eng):
                    for it in self.prog[e]:
                        if it[0] == 'wait':
                            k, v = it[1], it[2]
                            if k[0] == 'eng':
                                v = rank[(k, v)]
                            eng.wait_ge(self.sems[k], v)
                        else:
                            ins = it[1](eng)
                            k = it[2]
                            if k[0] == 'dma':
                                ins.then_inc(self.sems[k], 16)
                            elif (k, it[4]) in needed:
                                ins.then_inc(self.sems[k], 1)
                                self.n_inc[e] += 1
                return f
            block.tensor(mk('pe'))
            block.scalar(mk('act'))
            block.vector(mk('dve'))
            block.gpsimd(mk('pool'))
            block.sync(mk('sp'))


def build(nlayers=DEPTH, stop=None, debug=False):
    nc = bass.Bass("TRN2", target_bir_lowering=False)
    es = ExitStack()
    S = Sched(nc, es)
    op = S.op

    def dram(name, shape, dt=F32, kind="ExternalInput"):
        return nc.dram_tensor(name, list(shape), dt, kind=kind).ap()

    scratch_kind = "ExternalOutput" if debug else "Internal"
    x = dram("x", [2048, D])
    meta = dram("meta", [16, D])
    ln_in_g = dram("ln_in_g", [1, D])
    ln_in_b = dram("ln_in_b", [1, D])
    w_in = dram("w_in", [DEPTH, D, IN_W])
    gla_wg2 = dram("gla_wg2", [DEPTH, 16, 256])
    gla_bg = dram("gla_bg", [DEPTH, 1, 256])
    gla_norm_g = dram("gla_norm_g", [DEPTH, 1, 128])
    idx_k_g = dram("idx_k_g", [DEPTH, 1, 64])
    idx_k_b = dram("idx_k_b", [DEPTH, 1, 64])
    w_out = dram("w_out", [DEPTH, D, D])
    ln_g = dram("ln_g", [DEPTH, 1, D])
    ln_b = dram("ln_b", [DEPTH, 1, D])
    c_ident = dram("c_ident", [128, 128])
    c_tri_incl = dram("c_tri_incl", [128, 128])
    c_tri_strict = dram("c_tri_strict", [128, 128])
    c_maskA = dram("c_maskA", [128, 64])
    c_maskadd = dram("c_maskadd", [128, 128])
    c_tri01 = dram("c_tri01", [128, 128])
    c_pm = dram("c_pm", [64, 64])
    c_ct = dram("c_ct", [64, T])
    c_st = dram("c_st", [64, T])
    c_pow2 = dram("c_pow2", [128, NBIS + 1])
    c_cos_tok = dram("c_cos_tok", [T, 8])
    c_sin_tok = dram("c_sin_tok", [T, 8])
    out = dram("out", [2048, D], kind="ExternalOutput")
    hd = dram("hd", [T, D], kind=scratch_kind)
    ogd = dram("ogd", [NT, 128, 512], BF16, kind=scratch_kind)
    odd = dram("odd", [NT, 8, 64, 128], BF16, kind=scratch_kind)

    def sb(name, shape, dt=F32):
        return es.enter_context(nc.sbuf_tensor(name, list(shape), dt))

    hT = sb("hT", [128, 8, T], BF16)
    wbig = sb("wbig", [128, 8, 1552], BF16)
    wks = sb("wks", [128, 8, 320], BF16)
    wout = sb("wout", [128, 8, D], BF16)
    wg2 = sb("wg2", [16, 256])
    bg = sb("bg", [1, 256])
    gain_bc = sb("gain_bc", [128, 128])
    ikg_bc = sb("ikg_bc", [128, 64])
    ikb_bc = sb("ikb_bc", [128, 64])
    lng = sb("lng", [128, D])
    lnb = sb("lnb", [128, D])
    ident_f = sb("ident_f", [128, 128])
    ident_b = sb("ident_b", [128, 128], BF16)
    tri_incl = sb("tri_incl", [128, 128])
    tri_strict = sb("tri_strict", [128, 128])
    maskA = sb("maskA", [128, 64])
    maskadd = sb("maskadd", [128, 128])
    pm_b = sb("pm_b", [64, 64], BF16)
    ones_f = sb("ones_f", [128, 128])
    xt = [sb("xt%d" % i, [128, D]) for i in range(2)]
    hn = [sb("hn%d" % i, [128, D]) for i in range(2)]
    stats = sb("stats", [128, 2, 6])
    mv = sb("mv", [128, 2])
    rstd = sb("rstd", [128, 1])
    Fb = [sb("F%d" % i, [128, 512]) for i in range(6)]
    rec = Fb[4]
    Hb = [sb("H%d" % i, [128, 512], BF16) for i in range(6)]
    S_f = sb("S_f", [128, 256])
    S_b = sb("S_b", [128, 256], BF16)
    lrT = sb("lrT", [16, 128])
    ssq = sb("ssq", [128, 4])
    Vaug = sb("Vaug", [128, NT, 2, 65], BF16)
    akR = sb("akR", [64, 2, T], BF16)
    ikR = sb("ikR", [64, T], BF16)
    ctb = sb("ctb", [64, 512])
    stb = sb("stb", [64, 512])
    acc2 = [sb("acc%d" % i, [128, T]) for i in range(2)]
    sel2 = [sb("sel%d" % i, [128, T], BF16) for i in range(2)]
    selT = sb("selT", [128, T], BF16)
    xq2 = [sb("xq%d" % i, [64, 8, 128], BF16) for i in range(4)]
    xi2 = [sb("xi%d" % i, [64, 8, 128], BF16) for i in range(2)]
    szT2 = [sb("szT%d" % i, [64, 8, 128], BF16) for i in range(4)]
    iw2 = [sb("iw%d" % i, [128, 8]) for i in range(2)]
    tiny_t = sb("tiny_t", [128, 1])
    cs_t = sb("cs_t", [128, 16])
    ropet = sb("ropet", [128, 2, 4, 64])
    bs = sb("bs", [128, 16])
    wtab = sb("wtab", [128, NBIS + 1])
    pow2 = sb("pow2", [128, NBIS + 1])
    cbis = sb("cbis", [128, 4])
    odt = sb("odt", [64, 4, 128], BF16)
    odt2 = sb("odt2", [128, 4, 128], BF16)

    ps = [es.enter_context(nc.psum_tensor("ps%d" % i, [128, 512], F32)) for i in range(8)]
    PK = ["ps%d" % i for i in range(8)]

    def ld(dst, src, key, eng='sp'):
        return op(eng, lambda e, d=dst, s=src: e.dma_start(out=d, in_=s), writes=[key], dma=key)

    ld(ident_f[:], c_ident, 'ident_f')
    ld(ident_b[:], c_ident, 'ident_b', 'pool')
    ld(tri_incl[:], c_tri_incl, 'tri_incl')
    ld(tri_strict[:], c_tri_strict, 'tri_strict')
    ld(maskA[:], c_maskA, 'maskA')
    ld(maskadd[:], c_maskadd, 'maskadd')
    ld(pow2[:], c_pow2, 'pow2')
    op('pool', lambda e: e.memset(cbis[:, 0:1], 1.0 / 32), writes=['cbis'])
    op('pool', lambda e: e.memset(cbis[:, 1:2], 33.0 / 64), writes=['cbis'])
    op('pool', lambda e: e.memset(cbis[:, 2:3], TOPK - 0.5), writes=['cbis'])
    op('pool', lambda e: e.memset(cbis[:, 3:4], -1.0e29), writes=['cbis'])
    ld(pm_b[:], c_pm, 'pm_b', 'pool')
    op('pool', lambda e: e.memset(ones_f[:], 1.0), writes=['ones_f'])
    for c in range(8):
        op('pool', lambda e, c=c: e.memset(hT[:, c, 0:NPAD], 0.0), writes=[('hT', 0)])
    op('pool', lambda e: e.memset(Vaug[:].rearrange("p a b c -> p (a b c)"), 1.0), writes=['Vaug'])
    ld(lng[:], ln_in_g.to_broadcast([128, D]), 'lng')
    ld(lnb[:], ln_in_b.to_broadcast([128, D]), 'lnb')

    def layer_norm(src, skey, dst, dkey, geng='pool'):
        op('dve', lambda e: e.bn_stats(out=stats[:, 0, :], in_=src[:, 0:512]), reads=[skey], writes=['stats'])
        op('dve', lambda e: e.bn_stats(out=stats[:, 1, :], in_=src[:, 512:1024]), reads=[skey], writes=['stats'])
        op('dve', lambda e: e.bn_aggr(out=mv[:], in_=stats[:].rearrange("p a b -> p (a b)")),
           reads=['stats'], writes=['mv'])
        op('act', lambda e: e.activation(out=rstd[:], in_=mv[:, 1:2], func=AF.Sqrt, bias=eps_t[:, 0:1], scale=1.0),
           reads=['mv', 'eps'], writes=['rstd'])
        op('dve', lambda e: e.reciprocal(out=rstd[:], in_=rstd[:]), reads=['rstd'], writes=['rstd'])
        op('dve', lambda e: e.tensor_scalar(out=dst[:], in0=src[:], scalar1=mv[:, 0:1], scalar2=rstd[:, 0:1],
                                            op0=ALU.subtract, op1=ALU.mult),
           reads=[skey, 'mv', 'rstd'], writes=[dkey])
        op(geng, lambda e: e.tensor_tensor(out=dst[:], in0=dst[:], in1=lng[:], op=ALU.mult),
           reads=[dkey, 'lng'], writes=[dkey])
        op('pool', lambda e: e.tensor_tensor(out=dst[:], in0=dst[:], in1=lnb[:], op=ALU.add),
           reads=[dkey, 'lnb'], writes=[dkey])

    def build_hT(i, src, skey):
        lo = NPAD if i == 0 else 0
        for half in range(2):
            pk = PK[half]
            for cc in range(4):
                c = half * 4 + cc
                op('pe', lambda e, c=c, cc=cc, half=half: e.transpose(
                    out=ps[half][:, cc * 128:(cc + 1) * 128], in_=src[:, c * 128:(c + 1) * 128], identity=ident_f[:]),
                   reads=[skey, 'ident_f'], writes=[pk])
            op('act', lambda e, half=half: e.activation(
                out=hT[:, half * 4:half * 4 + 4, i * 128 + lo:(i + 1) * 128],
                in_=ps[half][:].rearrange("p (c t) -> p c t", c=4)[:, :, lo:128], func=AF.Copy),
               reads=[], writes=[pk, ('hT', i)])

    eps_t = sb("eps_t", [128, 1])
    op('pool', lambda e: e.memset(eps_t[:], LN_EPS), writes=['eps'])
    op('pool', lambda e: e.memset(tiny_t[:], 1.0e-30), writes=['tiny'])
    nb30k = sb("nb30k", [128, 1])
    op('pool', lambda e: e.memset(nb30k[:], -30000.0), writes=['nb30k'])
    cst = sb("cst", [128, 2])
    op('pool', lambda e: e.memset(cst[:, 0:1], 0.125), writes=['cst'])
    op('pool', lambda e: e.memset(cst[:, 1:2], ALPHA), writes=['cst'])

    import os as _os
    _np0 = int(_os.environ.get('DBG_NP0', NT))
    _dbg = _os.environ.get('DBG_SKIP', '')
    for i in range(_np0):
        j = i % 2
        xk, hk = 'xt%d' % j, 'hn%d' % j
        if i == 0:
            op('pool', lambda e: e.memset(xt[0][:], 0.0), writes=[xk])
            op('sp', lambda e: e.dma_start(out=xt[0][NPAD:128, :], in_=meta), reads=[], writes=[xk], dma=xk)
        else:
            op('sp', lambda e, i=i, j=j: e.dma_start(out=xt[j][:], in_=x[(i - 1) * 128:i * 128, :]),
               writes=[xk], dma=xk)
        if 'ln' not in _dbg:
            layer_norm(xt[j], xk, hn[j], hk, geng='dve')
        op('pool', lambda e, i=i, j=j: e.dma_start(out=hd[i * 128:(i + 1) * 128, :], in_=hn[j][:]),
           reads=[hk], writes=[('hd', i)], dma='hd')
        if 'ht' not in _dbg:
            build_hT(i, hn[j], hk)

    out_tokens = []

    def FK(k):
        return ['F%da' % k, 'F%db' % k]

    def HK(k):
        return ['H%da' % k, 'H%db' % k]

    Fb1 = [acc2[0][:, 512 * k:512 * (k + 1)] for k in range(4)] + [acc2[1][:, 0:512], acc2[1][:, 512:1024]]
    Hb1 = [sel2[0][:, 512 * k:512 * (k + 1)] for k in range(4)] + [sel2[1][:, 0:512], sel2[1][:, 512:1024]]
    Fb2 = [xt[0][:, 0:512], xt[0][:, 512:1024], xt[1][:, 0:512], xt[1][:, 512:1024],
           hn[0][:, 0:512], hn[0][:, 512:1024]]
    hn1b = hn[1][:].bitcast(BF16)
    Hb2 = [hn1b[:, 512 * k:512 * (k + 1)] for k in range(4)] + [odt2[:].rearrange("p c t -> p (c t)"),
                                                              sel2[1][:, 1024:1536]]
    lrT2 = [lrT, sb("lrT_b", [16, 128]), sb("lrT_c", [16, 128])]
    ssq2 = [ssq, sb("ssq_b", [128, 4]), sb("ssq_c", [128, 4])]
    fence_t = sb("fence_t", [128, 2])
    b1f = selT[:].bitcast(F32)
    ALIAS_KEYS = ['%s%d%s' % (c_, k, s) for c_ in 'GJMN' for k in range(6) for s in 'ab']
    OWNER_KEYS = ['acc0', 'acc1', 'sel0', 'sel1', 'xt0', 'xt1', 'hn0', 'hn1', 'odt2']

    _acut = int(_os.environ.get('DBG_ACUT', 99))
    _dmask = int(_os.environ.get('DBG_DMASK', 7))
    _na = int(_os.environ.get('DBG_NA', NT))

    def phaseA_tile(i):
        tc = slice(i * 128, (i + 1) * 128)
        hk = ('hT', i)
        q_ = i % 3
        FB, HB = ((Fb, Hb), (Fb1, Hb1), (Fb2, Hb2))[q_]
        qkT, F1, F2, F3, o_sb, F5 = FB
        v_bf, sz, H2, H3, og, ogT = HB
        fp, hp = (('F', 'H'), ('G', 'J'), ('M', 'N'))[q_]
        lrT_, ssq_ = lrT2[q_], ssq2[q_]
        lk, sk = 'lrT%d' % q_, 'ssq%d' % q_

        def fkk(k):
            return [fp + '%da' % k, fp + '%db' % k]

        def hkk(k):
            return [hp + '%da' % k, hp + '%db' % k]
        for c in range(4):
            for dc in range(8):
                op('pe', lambda e, c=c, dc=dc: e.matmul(
                    ps[0][:, c * 128:(c + 1) * 128], lhsT=wbig[:, dc, c * 128:(c + 1) * 128], rhs=hT[:, dc, tc],
                    start=(dc == 0), stop=(dc == 7)), reads=['wbig', hk], writes=[PK[0]])
        for dc in range(8):
            op('pe', lambda e, dc=dc: e.matmul(
                ps[1][0:16, 0:128], lhsT=wbig[:, dc, C_LR:C_LR + 16], rhs=hT[:, dc, tc],
                start=(dc == 0), stop=(dc == 7)), reads=['wbig', hk], writes=[PK[1]])
        op('act', lambda e: e.activation(out=qkT[:], in_=ps[0][:], func=AF.Copy), writes=[PK[0]] + fkk(0))
        op('act', lambda e: e.activation(out=lrT_[:], in_=ps[1][0:16, 0:128], func=AF.Copy), writes=[PK[1], lk])
        if _acut <= 1:
            return
        yield
        for (pi, n, off) in ((2, 256, C_GK), (3, 512, C_GV), (4, 512, C_GZ)):
            for dc in range(8):
                op('pe', lambda e, pi=pi, n=n, off=off, dc=dc: e.matmul(
                    ps[pi][:, 0:n], lhsT=hT[:, dc, tc], rhs=wbig[:, dc, off:off + n],
                    start=(dc == 0), stop=(dc == 7)), reads=['wbig', hk], writes=[PK[pi]])
        op('pe', lambda e: e.matmul(ps[1][:, 128:384], lhsT=lrT_[:], rhs=wg2[:], start=True, stop=False),
           reads=[lk, 'wg2'], writes=[PK[1]], rt='k32')
        op('pe', lambda e: e.matmul(ps[1][:, 128:384], lhsT=ones_f[0:1, :], rhs=bg[:], start=False, stop=True),
           reads=['ones_f', 'bg'], writes=[PK[1]], rt='k32')
        if _acut <= 2:
            return
        op('act', lambda e: e.activation(out=F1[:, 0:256], in_=ps[1][:, 128:384], func=AF.Exp, scale=-1.0),
           writes=[PK[1], (fp + '1a')])
        op('act', lambda e: e.activation(out=F1[:, 256:512], in_=F1[:, 0:256], func=AF.Ln, bias=ones_f[:, 0:1],
                                         scale=1.0), reads=[(fp + '1a'), 'ones_f'], writes=[(fp + '1b')])
        op('act', lambda e: e.activation(out=F2[:, 0:256], in_=ps[2][:, 0:256], func=AF.Copy),
           writes=[PK[2], (fp + '2a')])
        op('act', lambda e: e.activation(out=v_bf[:], in_=ps[3][:], func=AF.Copy), writes=[PK[3]] + hkk(0))
        op('act', lambda e: e.activation(out=sz[:], in_=ps[4][:], func=AF.Silu), writes=[PK[4]] + hkk(1))
        if _acut <= 3:
            return
        yield
        for fc in range(2):
            op('pe', lambda e, fc=fc: e.matmul(
                ps[5][:, fc * 128:(fc + 1) * 128], lhsT=F1[:, 256 + fc * 128:256 + (fc + 1) * 128],
                rhs=tri_incl[:], start=True, stop=True), reads=[(fp + '1b'), 'tri_incl'], writes=[PK[5]])
        op('pe', lambda e: e.matmul(ps[5][:, 256:512], lhsT=tri_strict[:], rhs=F1[:, 256:512],
                                    start=True, stop=True), reads=[(fp + '1b'), 'tri_strict'], writes=[PK[5]])
        op('act', lambda e: e.activation(out=F3[:, 0:256], in_=ps[5][:, 0:256], func=AF.Exp, scale=-1.0 / 16),
           writes=[PK[5], (fp + '3a')])
        op('act', lambda e: e.activation(out=F3[:, 256:512], in_=ps[5][:, 0:256], func=AF.Exp, scale=1.0 / 16),
           writes=[PK[5], (fp + '3b')])
        op('act', lambda e: e.activation(out=F2[:, 256:512], in_=ps[5][:, 256:512], func=AF.Exp,
                                         scale=-1.0 / 16), writes=[PK[5], (fp + '2b')])
        if _acut <= 4:
            return
        yield
        (op if _dmask & 1 else (lambda *a, **k: None))('dve', lambda e: e.scalar_tensor_tensor(out=H2[:, 0:256], in0=qkT[:, 0:256], scalar=cst[:, 0:1],
                                                   in1=F3[:, 0:256], op0=ALU.mult, op1=ALU.mult),
           reads=[(fp + '0a'), (fp + '3a'), 'cst'], writes=[(hp + '2a')])
        (op if _dmask & 2 else (lambda *a, **k: None))('dve', lambda e: e.tensor_tensor(out=H2[:, 256:512], in0=qkT[:, 256:512], in1=F3[:, 256:512],
                                            op=ALU.mult), reads=[(fp + '0b'), (fp + '3b')], writes=[(hp + '2b')])
        (op if _dmask & 4 else (lambda *a, **k: None))('dve', lambda e: e.tensor_tensor(out=H3[:, 0:256], in0=F2[:, 0:256], in1=F2[:, 256:512],
                                            op=ALU.mult), reads=[(fp + '2a'), (fp + '2b')], writes=[(hp + '3a')])
        if _acut <= 5:
            return
        yield
        for (c, h) in [(c, h) for h in (0, 2, 1, 3) for c in range(2)]:
            if True:
                fc, pb = h // 2, (h % 2) * 64
                c0 = fc * 128 + 64 * c
                op('pe', lambda e, c=c, h=h, pb=pb, c0=c0: e.matmul(
                    ps[6][64 * c:64 * c + 64, h * 64:(h + 1) * 64],
                    lhsT=H2[pb:pb + 64, 256 + c0:256 + c0 + 64], rhs=H2[pb:pb + 64, c0:c0 + 64],
                    start=True, stop=True), reads=[(hp + '2a'), (hp + '2b')], writes=[PK[6]], rt=pb)
        op('dve', lambda e: e.tensor_tensor(
            out=H3[:, 256:512].rearrange("p (h t) -> p h t", h=4),
            in0=ps[6][:, 0:256].rearrange("p (h t) -> p h t", h=4),
            in1=maskA[:].unsqueeze(1).to_broadcast([128, 4, 64]), op=ALU.mult),
           reads=['maskA'], writes=[PK[6], (hp + '3b')])
        if _acut <= 6:
            return
        yield
        for c in range(2):
            r0 = 64 * c

            def o_out(h, r0=r0):
                if h % 2 == 0:
                    return ps[7][r0:r0 + 64, h * 128:(h + 1) * 128], PK[7]
                return ps[6][r0:r0 + 64, (h // 2) * 128:(h // 2 + 1) * 128], PK[6]

            for hp_ in range(2):
                hs = (2 * hp_, 2 * hp_ + 1)
                for h in hs:
                    fc, pb = h // 2, (h % 2) * 64
                    c0 = fc * 128 + 64 * c
                    oap, okey = o_out(h)
                    op('pe', lambda e, oap=oap, pb=pb, c0=c0, fc=fc: e.matmul(
                        oap, lhsT=H2[pb:pb + 64, c0:c0 + 64],
                        rhs=S_b[pb:pb + 64, fc * 128:(fc + 1) * 128], start=True, stop=False),
                       reads=[(hp + '2a'), 'S_b'], writes=[okey], rt=pb)
                for h in hs:
                    oap, okey = o_out(h)
                    op('pe', lambda e, oap=oap, r0=r0, h=h: e.matmul(
                        oap, lhsT=H3[r0:r0 + 64, 256 + h * 64:256 + (h + 1) * 64],
                        rhs=v_bf[r0:r0 + 64, h * 128:(h + 1) * 128], start=False, stop=True),
                       reads=[(hp + '3b')] + hkk(0), writes=[okey], rt=r0)
            for h in range(4):
                fc, pb = h // 2, (h % 2) * 64
                op('pe', lambda e, r0=r0, h=h, pb=pb, fc=fc: e.matmul(
                    ps[6][pb:pb + 64, 256 + fc * 128:256 + (fc + 1) * 128],
                    lhsT=H3[r0:r0 + 64, h * 64:(h + 1) * 64], rhs=v_bf[r0:r0 + 64, h * 128:(h + 1) * 128],
                    start=True, stop=True), reads=[(hp + '3a')] + hkk(0), writes=[PK[6]], rt=r0)
            for fc in range(2):
                op('dve', lambda e, fc=fc, c=c: e.scalar_tensor_tensor(
                    out=S_f[:, fc * 128:(fc + 1) * 128], in0=S_f[:, fc * 128:(fc + 1) * 128],
                    scalar=F3[:, fc * 128 + 64 * c + 63:fc * 128 + 64 * c + 64],
                    in1=ps[6][:, 256 + fc * 128:256 + (fc + 1) * 128], op0=ALU.mult, op1=ALU.add),
                   reads=[(fp + '3a')], writes=[PK[6], 'S_f'])
            op('act', lambda e: e.activation(out=S_b[:], in_=S_f[:], func=AF.Copy), reads=['S_f'], writes=['S_b'])
        if _acut <= 7:
            return
        yield
        o_v = o_sb[:].rearrange("p (a b) -> p a b", a=2)
        op('act', lambda e: e.activation(out=o_v[:, :, 0:128],
                                         in_=ps[7][:].rearrange("p (a b) -> p a b", a=2)[:, :, 0:128], func=AF.Copy),
           writes=[PK[7]] + fkk(4))
        op('act', lambda e: e.activation(out=o_v[:, :, 128:256],
                                         in_=ps[6][:, 0:256].rearrange("p (a b) -> p a b", a=2), func=AF.Copy),
           writes=[PK[6]] + fkk(4))
        op('dve', lambda e: e.tensor_tensor(out=F5[:], in0=o_sb[:], in1=o_sb[:], op=ALU.mult),
           reads=fkk(4), writes=fkk(5))
        op('dve', lambda e: e.tensor_reduce(out=ssq_[:], in_=F5[:].rearrange("p (h d) -> p h d", h=4),
                                            axis=AX.X, op=ALU.add), reads=fkk(5), writes=[sk])
        op('dve', lambda e: e.tensor_scalar(out=ssq_[:], in0=ssq_[:], scalar1=1.0 / 128, scalar2=LN_EPS,
                                            op0=ALU.mult, op1=ALU.add), reads=[sk], writes=[sk])
        op('act', lambda e: e.activation(out=ssq_[:], in_=ssq_[:], func=AF.Sqrt), reads=[sk], writes=[sk])
        op('dve', lambda e: e.reciprocal(out=ssq_[:], in_=ssq_[:]), reads=[sk], writes=[sk])
        op('dve', lambda e: e.tensor_tensor(
            out=F5[:].rearrange("p (h d) -> p h d", h=4), in0=o_sb[:].rearrange("p (h d) -> p h d", h=4),
            in1=ssq_[:].unsqueeze(2).to_broadcast([128, 4, 128]), op=ALU.mult),
           reads=fkk(4) + [sk], writes=fkk(5))
        op('pool', lambda e: e.tensor_tensor(
            out=F5[:].rearrange("p (h d) -> p h d", h=4), in0=F5[:].rearrange("p (h d) -> p h d", h=4),
            in1=gain_bc[:].unsqueeze(1).to_broadcast([128, 4, 128]), op=ALU.mult),
           reads=fkk(5) + ['gain_bc'], writes=fkk(5))
        op('pool', lambda e: e.tensor_tensor(out=og[:], in0=F5[:], in1=sz[:], op=ALU.mult),
           reads=fkk(5) + hkk(1), writes=hkk(4))
        if _acut <= 8:
            return
        yield
        psb0 = ps[0][:].bitcast(BF16)
        for h in range(4):
            op('pe', lambda e, h=h: e.transpose(out=psb0[:, h * 128:(h + 1) * 128],
                                                in_=og[:, h * 128:(h + 1) * 128], identity=ident_b[:]),
               reads=hkk(4) + ['ident_b'], writes=[PK[0]])
        op('act', lambda e: e.activation(out=ogT[:], in_=psb0[:, 0:512], func=AF.Copy), writes=[PK[0]] + hkk(5))
        op('sp', lambda e: e.dma_start(out=ogd[i], in_=ogT[:]), reads=hkk(5), writes=[('ogd', i)],
           dma='ogd')

    def phaseB1_tile(i):
        tc = slice(i * 128, (i + 1) * 128)
        hk = ('hT', i)
        for (c0, n, off) in ((0, 128, 128), (128, 64, 256)):
            for dc in range(8):
                op('pe', lambda e, c0=c0, n=n, off=off, dc=dc: e.matmul(
                    ps[1][:, c0:c0 + n], lhsT=hT[:, dc, tc], rhs=wks[:, dc, off:off + n],
                    start=(dc == 0), stop=(dc == 7)), reads=['wks', hk], writes=[PK[1]])
        op('act', lambda e: e.activation(out=Vaug[:, i, :, 0:64],
                                         in_=ps[1][:, 0:128].rearrange("p (g d) -> p g d", g=2),
                                         func=AF.Copy), writes=[PK[1], 'Vaug'])
        ikf, ikn = b1f[:, 0:64], b1f[:, 64:128]
        op('act', lambda e: e.activation(out=ikf, in_=ps[1][:, 128:192], func=AF.Copy), writes=[PK[1], 'selT'])
        op('dve', lambda e: e.bn_stats(out=stats[:, 0, :], in_=ikf), reads=['selT'], writes=['stats'])
        op('dve', lambda e: e.bn_aggr(out=mv[:], in_=stats[:, 0, :]), reads=['stats'], writes=['mv'])
        op('act', lambda e: e.activation(out=rstd[:], in_=mv[:, 1:2], func=AF.Sqrt, bias=eps_t[:, 0:1], scale=1.0),
           reads=['mv', 'eps'], writes=['rstd'])
        op('dve', lambda e: e.reciprocal(out=rstd[:], in_=rstd[:]), reads=['rstd'], writes=['rstd'])
        op('dve', lambda e: e.tensor_scalar(out=ikn, in0=ikf, scalar1=mv[:, 0:1], scalar2=rstd[:, 0:1],
                                            op0=ALU.subtract, op1=ALU.mult),
           reads=['selT', 'mv', 'rstd'], writes=['selT'])
        op('pool', lambda e: e.tensor_tensor(out=ikn, in0=ikn, in1=ikg_bc[:], op=ALU.mult),
           reads=['selT', 'ikg_bc'], writes=['selT'])
        op('pool', lambda e: e.tensor_tensor(out=ikn, in0=ikn, in1=ikb_bc[:], op=ALU.add),
           reads=['selT', 'ikb_bc'], writes=['selT'])
        yield
        op('pe', lambda e: e.transpose(out=ps[2][0:64, 0:128], in_=ikn, identity=ident_f[:]),
           reads=['selT', 'ident_f'], writes=[PK[2]])
        op('act', lambda e: e.activation(out=ikR[:, tc], in_=ps[2][0:64, 0:128], func=AF.Copy),
           writes=[PK[2], ('ikR', i // 4)])

    def phaseB1_block(bi):
        b0 = bi * 512
        n = min(512, T - b0)
        bc = slice(b0, b0 + n)
        hks = [('hT', t) for t in range(b0 // 128, (b0 + n) // 128)]
        op('sp', lambda e: e.dma_start(out=ctb[:, 0:n], in_=c_ct[:, bc]), writes=['ctb'], dma='ctb')
        op('sp', lambda e: e.dma_start(out=stb[:, 0:n], in_=c_st[:, bc]), writes=['stb'], dma='stb')
        for g in range(2):
            for dc in range(8):
                op('pe', lambda e, g=g, dc=dc: e.matmul(
                    ps[3 + g][0:64, 0:n], lhsT=wks[:, dc, g * 64:(g + 1) * 64], rhs=hT[:, dc, bc],
                    start=(dc == 0), stop=(dc == 7)), reads=['wks'] + hks, writes=[PK[3 + g]])
            op('act', lambda e, g=g: e.activation(out=akR[:, g, bc], in_=ps[3 + g][0:64, 0:n], func=AF.Copy),
               writes=[PK[3 + g], ('akR', g, bi)])
            yield
        for (X, xk) in ((akR[:, 0, bc], ('akR', 0, bi)), (akR[:, 1, bc], ('akR', 1, bi)),
                        (ikR[:, bc], ('ikR', bi))):
            tmp = b1f[0:64, 128:128 + n]
            op('pe', lambda e, X=X: e.matmul(ps[5][0:64, 0:n], lhsT=pm_b[:], rhs=X, start=True, stop=True),
               reads=[xk, 'pm_b'], writes=[PK[5]], rt=0)
            op('dve', lambda e, tmp=tmp: e.tensor_tensor(out=tmp, in0=ps[5][0:64, 0:n], in1=stb[:, 0:n],
                                                          op=ALU.mult),
               reads=['stb'], writes=[PK[5], 'selT'])
            op('pool', lambda e, X=X: e.tensor_tensor(out=X, in0=X, in1=ctb[:, 0:n], op=ALU.mult),
               reads=[xk, 'ctb'], writes=[xk])
            op('pool', lambda e, X=X, tmp=tmp: e.tensor_tensor(out=X, in0=X, in1=tmp, op=ALU.add),
               reads=[xk, 'selT'], writes=[xk])
            yield

    ct_t, st_t = Fb[2][0:64, 0:128], Fb[2][0:64, 256:384]

    def b2_front(i):
        p = i % 2
        p3 = i % 4
        xq, xi, szT, iw_sb = xq2[p3], xi2[p], szT2[p3], iw2[p]
        tc = slice(i * 128, (i + 1) * 128)
        hk = ('hT', i)
        aq_tok, iq_tok, az_tok = Fb[2], Fb[3], Hb[5]
        op('sp', lambda e: e.dma_start(out=cs_t[:, 0:8], in_=c_cos_tok[tc, :]), writes=['cs_t'], dma='cs_t')
        op('sp', lambda e: e.dma_start(out=cs_t[:, 8:16], in_=c_sin_tok[tc, :]), writes=['cs_t'], dma='cs_t')
        for (off, pi, dst_, dkeys) in ((0, 0, aq_tok, FK(2)), (512, 1, iq_tok, FK(3))):
            for dc in range(8):
                op('pe', lambda e, pi=pi, dc=dc, off=off: e.matmul(
                    ps[pi][:, :], lhsT=hT[:, dc, tc], rhs=wbig[:, dc, off:off + 512],
                    start=(dc == 0), stop=(dc == 7)), reads=['wbig', hk], writes=[PK[pi]])
            op('act', lambda e, pi=pi, dst_=dst_: e.activation(out=dst_[:], in_=ps[pi][:], func=AF.Copy),
               writes=[PK[pi]] + dkeys)
            yield
        for dc in range(8):
            op('pe', lambda e, dc=dc: e.matmul(ps[2][:, :], lhsT=hT[:, dc, tc], rhs=wbig[:, dc, 1024:1536],
                                               start=(dc == 0), stop=(dc == 7)),
               reads=['wbig', hk], writes=[PK[2]])
        op('act', lambda e: e.activation(out=az_tok[:], in_=ps[2][:], func=AF.Silu), writes=[PK[2]] + HK(5))
        for dc in range(8):
            op('pe', lambda e, dc=dc: e.matmul(ps[0][:, 0:8], lhsT=hT[:, dc, tc], rhs=wbig[:, dc, 1536:1544],
                                               start=(dc == 0), stop=(dc == 7)),
               reads=['wbig', hk], writes=[PK[0]])
        op('act', lambda e: e.activation(out=iw_sb[:], in_=ps[0][:, 0:8], func=AF.Copy), writes=[PK[0], ('iw', p)])
        yield
        cosb = cs_t[:, 0:8].unsqueeze(1).to_broadcast([128, 8, 8])
        sinb = cs_t[:, 8:16].unsqueeze(1).to_broadcast([128, 8, 8])
        for xi_, (X, xkeys) in enumerate(((aq_tok, FK(2)), (iq_tok, FK(3)))):
            Xv = X[:].rearrange("p (h d) -> p h d", h=8)
            x1, x2 = Xv[:, :, 0:8], Xv[:, :, 8:16]
            tk = ('ropet', xi_)
            t1, t2, t3, t4 = [ropet[:, xi_, k, :].rearrange("p (h d) -> p h d", h=8) for k in range(4)]
            for (o_, a_, b_) in ((t1, x1, cosb), (t2, x2, sinb), (t3, x2, cosb), (t4, x1, sinb)):
                op('pool', lambda e, o_=o_, a_=a_, b_=b_: e.tensor_tensor(out=o_, in0=a_, in1=b_, op=ALU.mult),
                   reads=xkeys + ['cs_t'], writes=[tk])
            op('pool', lambda e, x1=x1, t1=t1, t2=t2: e.tensor_tensor(out=x1, in0=t1, in1=t2, op=ALU.subtract),
               reads=[tk], writes=xkeys)
            op('pool', lambda e, x2=x2, t3=t3, t4=t4: e.tensor_tensor(out=x2, in0=t3, in1=t4, op=ALU.add),
               reads=[tk], writes=xkeys)
            yield
        psb1 = ps[1][:].bitcast(BF16)
        for (X, xkeys, dst, dk_, pp, isbf) in ((aq_tok, FK(2), xq, 'xq', p3, False), (iq_tok, FK(3), xi, 'xi', p, False),
                                               (az_tok, HK(5), szT, 'szT', p3, True)):
            for half in range(2):
                for hh in range(4):
                    h = half * 4 + hh
                    if isbf:
                        op('pe', lambda e, X=X, h=h, hh=hh: e.transpose(
                            out=psb1[0:64, hh * 128:(hh + 1) * 128], in_=X[:, h * 64:(h + 1) * 64],
                            identity=ident_b[:]), reads=xkeys + ['ident_b'], writes=[PK[1]])
                    else:
                        op('pe', lambda e, X=X, h=h, hh=hh: e.transpose(
                            out=ps[0][0:64, hh * 128:(hh + 1) * 128], in_=X[:, h * 64:(h + 1) * 64],
                            identity=ident_f[:]), reads=xkeys + ['ident_f'], writes=[PK[0]])
                if isbf:
                    op('act', lambda e, dst=dst, half=half: e.activation(
                        out=dst[:, half * 4:half * 4 + 4, :],
                        in_=psb1[0:64, 0:512].rearrange("p (h t) -> p h t", h=4), func=AF.Copy),
                       writes=[PK[1], (dk_, pp, half)])
                else:
                    op('act', lambda e, dst=dst, half=half: e.activation(
                        out=dst[:, half * 4:half * 4 + 4, :],
                        in_=ps[0][0:64, :].rearrange("p (h t) -> p h t", h=4), func=AF.Copy),
                       writes=[PK[0], (dk_, pp, half)])
                yield

    def b2_scores(i):
        p = i % 2
        xi, iw_sb, acc = xi2[p], iw2[p], acc2[p]
        ak_ = 'acc%d' % p
        W = 128 * (i + 1)
        for b5 in range(0, W, 512):
            n = min(512, W - b5)
            ikk = [('ikR', b5 // 512)]
            for j in range(8):
                pi = 3 + (j % 2)
                r = Fb[j % 2]
                rk = FK(j % 2)
                op('pe', lambda e, pi=pi, j=j, b5=b5, n=n: e.matmul(
                    ps[pi][:, 0:n], lhsT=xi[:, j, :], rhs=ikR[:, b5:b5 + n], start=True, stop=True),
                   reads=[('xi', p, j // 4)] + ikk, writes=[PK[pi]], rt=0)
                op('act', lambda e, pi=pi, r=r, n=n: e.activation(out=r[:, 0:n], in_=ps[pi][:, 0:n], func=AF.Relu),
                   writes=[PK[pi]] + rk)
                if j == 0:
                    op('dve', lambda e, r=r, b5=b5, n=n: e.tensor_scalar(
                        out=acc[:, b5:b5 + n], in0=r[:, 0:n], scalar1=iw_sb[:, 0:1], scalar2=None, op0=ALU.mult),
                       reads=rk + [('iw', p)], writes=[ak_])
                else:
                    op('dve', lambda e, r=r, b5=b5, n=n, j=j: e.scalar_tensor_tensor(
                        out=acc[:, b5:b5 + n], in0=r[:, 0:n], scalar=iw_sb[:, j:j + 1], in1=acc[:, b5:b5 + n],
                        op0=ALU.mult, op1=ALU.add), reads=rk + [('iw', p), ak_], writes=[ak_])
                yield

    def b2_topk(i):
        p = i % 2
        acc, sel = acc2[p], sel2[p]
        ak_, sk_ = 'acc%d' % p, 'sel%d' % p
        tc = slice(i * 128, (i + 1) * 128)
        W = 128 * (i + 1)
        if W <= TOPK:
            op('dve', lambda e: e.tensor_tensor(out=acc[:, tc], in0=acc[:, tc], in1=maskadd[:], op=ALU.add),
               reads=[ak_, 'maskadd'], writes=[ak_])
            op('dve', lambda e: e.memset(acc[:, 0:NPAD], NEG), reads=[], writes=[ak_])
            op('dve', lambda e: e.tensor_scalar(out=sel[:, 0:W], in0=acc[:, 0:W], scalar1=cbis[:, 3:4],
                                                scalar2=None, op0=ALU.is_ge), reads=[ak_, 'cbis'], writes=[sk_])
            yield
            return
        op('dve', lambda e: e.tensor_reduce(out=bs[:, 0:1], in_=acc[:, 0:W], axis=AX.X, op=ALU.max),
           reads=[ak_], writes=['bs'])
        op('dve', lambda e: e.tensor_reduce(out=bs[:, 1:2], in_=acc[:, 0:W], axis=AX.X, op=ALU.min),
           reads=[ak_], writes=['bs'])
        op('dve', lambda e: e.tensor_tensor(out=acc[:, tc], in0=acc[:, tc], in1=maskadd[:], op=ALU.add),
           reads=[ak_, 'maskadd'], writes=[ak_])
        op('dve', lambda e: e.memset(acc[:, 0:NPAD], NEG), reads=[], writes=[ak_])
        op('dve', lambda e: e.tensor_tensor(out=bs[:, 2:3], in0=bs[:, 0:1], in1=bs[:, 1:2], op=ALU.subtract),
           reads=['bs'], writes=['bs'])
        op('dve', lambda e: e.tensor_scalar(out=bs[:, 3:5], in0=bs[:, 2:3].to_broadcast([128, 2]),
                                            scalar1=cbis[:, 0:1], scalar2=None, op0=ALU.mult),
           reads=['bs', 'cbis'], writes=['bs'])
        op('dve', lambda e: e.tensor_scalar(out=bs[:, 4:5], in0=bs[:, 2:3], scalar1=cbis[:, 1:2], scalar2=None,
                                            op0=ALU.mult), reads=['bs', 'cbis'], writes=['bs'])
        op('dve', lambda e: e.tensor_tensor(out=bs[:, 5:6], in0=bs[:, 1:2], in1=bs[:, 3:4], op=ALU.subtract),
           reads=['bs'], writes=['bs'])
        op('dve', lambda e: e.tensor_tensor(out=bs[:, 6:7], in0=bs[:, 5:6], in1=bs[:, 4:5], op=ALU.add),
           reads=['bs'], writes=['bs'])
        op('dve', lambda e: e.tensor_scalar(out=wtab[:], in0=pow2[:], scalar1=bs[:, 4:5], scalar2=None,
                                            op0=ALU.mult), reads=['bs', 'pow2'], writes=['wtab'])
        m_ = bs[:, 6:7]
        for k in range(NBIS):
            op('dve', lambda e: e.tensor_scalar(out=sel[:, 0:W], in0=acc[:, 0:W], scalar1=m_, scalar2=None,
                                                op0=ALU.is_ge, op1=ALU.add, accum_out=bs[:, 7:8]),
               reads=[ak_, 'bs'], writes=[sk_, 'bs'])
            op('dve', lambda e, k=k: e.tensor_scalar(out=bs[:, 8:9], in0=bs[:, 7:8], scalar1=cbis[:, 2:3],
                                                     scalar2=wtab[:, k:k + 1], op0=ALU.is_ge, op1=ALU.mult),
               reads=['bs', 'cbis', 'wtab'], writes=['bs'])
            op('dve', lambda e, k=k: e.scalar_tensor_tensor(out=m_, in0=bs[:, 8:9], scalar=wtab[:, k + 1:k + 2],
                                                            in1=m_, op0=ALU.subtract, op1=ALU.add),
               reads=['bs', 'wtab'], writes=['bs'])
            yield
        op('dve', lambda e: e.tensor_tensor(out=bs[:, 9:10], in0=m_, in1=wtab[:, NBIS:NBIS + 1], op=ALU.subtract),
           reads=['bs', 'wtab'], writes=['bs'])
        op('dve', lambda e: e.tensor_scalar(out=sel[:, 0:W], in0=acc[:, 0:W], scalar1=bs[:, 9:10], scalar2=None,
                                            op0=ALU.is_ge), reads=[ak_, 'bs'], writes=[sk_])

    def b2_back(i):
        p = i % 2
        p3 = i % 4
        xq, szT, sel = xq2[p3], szT2[p3], sel2[p]
        sk_ = 'sel%d' % p
        psb5 = ps[5][:].bitcast(BF16)
        for kb0 in range(0, i + 1, 4):
            nb = min(4, i + 1 - kb0)
            for kk in range(nb):
                kb = kb0 + kk
                op('pe', lambda e, kk=kk, kb=kb: e.transpose(
                    out=psb5[:, kk * 128:(kk + 1) * 128], in_=sel[:, kb * 128:(kb + 1) * 128], identity=ident_b[:]),
                   reads=[sk_, 'ident_b'], writes=[PK[5]])
            op('act', lambda e, kb0=kb0, nb=nb: e.activation(
                out=selT[:, kb0 * 128:(kb0 + nb) * 128], in_=psb5[:, 0:nb * 128], func=AF.Identity,
                bias=nb30k[:, 0:1], scale=30000.0),
               reads=['nb30k'], writes=[PK[5], 'selT'])
            yield
        for g in range(2):
            for kb0 in range(0, i + 1, 2):
                kbs = [kb for kb in (kb0, kb0 + 1) if kb <= i]
                for kb in kbs:
                    pi = 6 + (kb % 2)
                    op('pe', lambda e, pi=pi, g=g, kb=kb: e.matmul(
                        ps[pi][:, :], lhsT=akR[:, g, kb * 128:(kb + 1) * 128],
                        rhs=xq[:, 4 * g:4 * g + 4, :].rearrange("p h t -> p (h t)"), start=True, stop=False),
                       reads=[('akR', g, kb // 4), ('xq', p3, g)], writes=[PK[pi]], rt=0)
                for kb in kbs:
                    pi = 6 + (kb % 2)
                    op('pe', lambda e, pi=pi, kb=kb: e.matmul(
                        ps[pi][:, :].rearrange("p (h t) -> p h t", h=4), lhsT=ident_b[:],
                        rhs=selT[:, kb * 128:(kb + 1) * 128].unsqueeze(1).to_broadcast([128, 4, 128]),
                        start=False, stop=True), reads=['selT', 'ident_b'], writes=[PK[pi]])
                for kb in kbs:
                    pi = 6 + (kb % 2)
                    E_ = Hb[kb % 2]
                    ek = HK(kb % 2)
                    op('act', lambda e, pi=pi, E_=E_: e.activation(out=E_[:], in_=ps[pi][:], func=AF.Exp,
                                                                   scale=0.125), writes=[PK[pi]] + ek)
                for kb in kbs:
                    E_ = Hb[kb % 2]
                    ek = HK(kb % 2)
                    op('pe', lambda e, g=g, kb=kb, E_=E_: e.matmul(
                        ps[5][0:65, :], lhsT=Vaug[:, kb, g, :], rhs=E_[:], start=(kb == 0), stop=(kb == i)),
                       reads=['Vaug'] + ek, writes=[PK[5]])
                yield
            op('act', lambda e: e.activation(out=rec[64:65, :], in_=ps[5][64:65, :], func=AF.Ln,
                                             bias=tiny_t[64:65, 0:1], scale=1.0),
               reads=['tiny'], writes=[PK[5], 'rec'])
            op('act', lambda e: e.activation(out=rec[64:65, :], in_=rec[64:65, :], func=AF.Exp, scale=-1.0),
               reads=['rec'], writes=['rec'])
            bc_ = Fb[4][0:64, :]
            on_ = Fb[5][0:64, :]
            op('act', lambda e, on_=on_: e.activation(out=on_, in_=ps[5][0:64, :], func=AF.Copy),
               writes=[PK[5]] + FK(5))
            op('pe', lambda e: e.matmul(ps[2][0:64, :], lhsT=ones_f[64:65, 0:64], rhs=rec[64:65, :],
                                        start=True, stop=True), reads=['rec', 'ones_f'], writes=[PK[2]], rt='k32b')
            op('act', lambda e, bc_=bc_: e.activation(out=bc_, in_=ps[2][0:64, :], func=AF.Copy),
               writes=[PK[2]] + FK(4))
            op('pool', lambda e, bc_=bc_, on_=on_: e.tensor_tensor(out=on_, in0=on_, in1=bc_, op=ALU.mult),
               reads=FK(4) + FK(5), writes=FK(5))
            op('pool', lambda e, g=g, on_=on_: e.tensor_tensor(
                out=odt[:], in0=on_.rearrange("p (h t) -> p h t", h=4), in1=szT[:, 4 * g:4 * g + 4, :],
                op=ALU.mult), reads=FK(5) + [('szT', p3, g)], writes=['odt'])
            op('pool', lambda e, g=g: e.dma_start(
                out=odd[i, 4 * g:4 * g + 4].rearrange("h d t -> d h t"), in_=odt[:]),
               reads=['odt'], writes=[('odd', i, g)], dma='odd')
            yield

    def phaseC_tile(i, last):
        j = i % 2
        xk, hk2 = 'xt%d' % j, 'hn%d' % j
        ogt = Hb[4]
        op('sp', lambda e: e.dma_start(out=ogt[:], in_=ogd[i]), reads=[('ogd', i)], writes=HK(4), dma='H4')
        op('sp', lambda e: e.dma_start(out=odt2[:], in_=odd[i].rearrange("(c r) d t -> (r d) c t", r=2)),
           reads=[('odd', i, 0), ('odd', i, 1)], writes=['odt2'], dma='odt2')
        op('sp', lambda e: e.dma_start(out=xt[j][:], in_=hd[i * 128:(i + 1) * 128, :]),
           reads=[('hd', i)], writes=[xk], dma=xk)
        yield
        for nb in range(2):
            cols = slice(nb * 512, (nb + 1) * 512)
            for c in range(4):
                op('pe', lambda e, nb=nb, c=c, cols=cols: e.matmul(
                    ps[nb][:, :], lhsT=ogt[:, c * 128:(c + 1) * 128], rhs=wout[:, c, cols],
                    start=(c == 0), stop=False), reads=HK(4) + ['wout'], writes=[PK[nb]])
            for c in range(4):
                op('pe', lambda e, nb=nb, c=c, cols=cols: e.matmul(
                    ps[nb][:, :], lhsT=odt2[:, c, :], rhs=wout[:, 4 + c, cols],
                    start=False, stop=(c == 3)), reads=['odt2', 'wout'], writes=[PK[nb]])
            op('dve', lambda e, nb=nb, cols=cols: e.scalar_tensor_tensor(
                out=xt[j][:, cols], in0=xt[j][:, cols], scalar=cst[:, 1:2], in1=ps[nb][:, :],
                op0=ALU.mult, op1=ALU.add), reads=[xk, 'cst'], writes=[PK[nb], xk])
            yield
        if i >= 1:
            phaseC_b(i - 1, last)
        layer_norm(xt[j], xk, hn[j], hk2)

    def phaseC_b(i, last):
        j = i % 2
        hk2 = 'hn%d' % j
        if last:
            if i >= 1:
                out_tokens.append(op('pool', lambda e: e.dma_start(
                    out=out[(i - 1) * 128:i * 128, :], in_=hn[j][:]), reads=[hk2], writes=[('out', i)],
                    dma='out'))
        else:
            op('pool', lambda e: e.dma_start(out=hd[i * 128:(i + 1) * 128, :], in_=hn[j][:]),
               reads=[hk2], writes=[('hd', i)], dma='hd')
            build_hT(i, hn[j], hk2)

    gla_prefetched = set()

    def layer(l):
        w_l = w_in[l].rearrange("(c p) n -> p c n", p=128)
        wo_l = w_out[l].rearrange("(c p) n -> p c n", p=128)
        if l not in gla_prefetched:
            op('pool', [lambda e, c=c: e.dma_start(out=wbig[:, c, 0:1552], in_=w_l[:, c, 0:1552]) for c in range(8)],
               writes=['wbig'], dma='wbig')
        op('pool', [lambda e, c=c: e.dma_start(out=wks[:, c, 0:128], in_=w_l[:, c, C_AK:C_AK + 128]) for c in range(8)]
           + [lambda e, c=c: e.dma_start(out=wks[:, c, 128:256], in_=w_l[:, c, C_AV:C_AV + 128]) for c in range(8)]
           + [lambda e, c=c: e.dma_start(out=wks[:, c, 256:320], in_=w_l[:, c, C_IK:C_IK + 64]) for c in range(8)],
           writes=['wks'], dma='wks')
        op('pool', [lambda e, c=c: e.dma_start(out=wout[:, c, :], in_=wo_l[:, c, :]) for c in range(8)],
           writes=['wout'], dma='wout')
        ld(wg2[:], gla_wg2[l], 'wg2')
        ld(bg[:], gla_bg[l], 'bg')
        ld(gain_bc[:], gla_norm_g[l].to_broadcast([128, 128]), 'gain_bc')
        ld(ikg_bc[:], idx_k_g[l].to_broadcast([128, 64]), 'ikg_bc')
        ld(ikb_bc[:], idx_k_b[l].to_broadcast([128, 64]), 'ikb_bc')
        ld(lng[:], ln_g[l].to_broadcast([128, D]), 'lng')
        ld(lnb[:], ln_b[l].to_broadcast([128, D]), 'lnb')
        op('pool', lambda e: e.memset(S_f[:], 0.0), writes=['S_f'])
        op('pool', lambda e: e.memset(S_b[:], 0.0), writes=['S_b'])
        op('pool', lambda e: e.memset(fence_t[:, 0:1], 0.0), writes=OWNER_KEYS + ALIAS_KEYS + ['fence'])
        gensA = [phaseA_tile(i) for i in range(_na)]
        qside_loaded = []

        def load_qside():
            op('pool', [lambda e, c=c, o=o, s=s, n=n: e.dma_start(out=wbig[:, c, o:o + n], in_=w_l[:, c, s:s + n])
                        for c in range(8) for (o, s, n) in ((0, C_AQ, 512), (512, C_IQ, 512), (1024, C_AZ, 512),
                                                            (1536, C_IW, 8))],
               writes=['wbig'], dma='wbig')

        def b1_lane():
            for i in range(NT):
                yield from phaseB1_tile(i)
                if i % 4 == 3:
                    yield from phaseB1_block(i // 4)
            yield from phaseB1_block(4)
        laneB = b1_lane() if stop != 'A' else iter(())
        activeA, nxt, done_steps = [], 0, {}
        laneB_live = True
        while nxt < len(gensA) or activeA or laneB_live:
            if len(activeA) < 3 and nxt < len(gensA) and (not activeA or done_steps[activeA[-1]] >= 3):
                activeA.append(nxt)
                done_steps[nxt] = 0
                nxt += 1
            for gi in list(activeA):
                try:
                    next(gensA[gi])
                    done_steps[gi] += 1
                except StopIteration:
                    activeA.remove(gi)
                if gi == len(gensA) - 1 and done_steps[gi] == 2 and not qside_loaded and stop not in ('A', 'B1'):
                    load_qside()
                    qside_loaded.append(True)
            if laneB_live:
                try:
                    next(laneB)
                except StopIteration:
                    laneB_live = False
        op('pool', lambda e: e.memset(fence_t[:, 1:2], 0.0), writes=OWNER_KEYS + ALIAS_KEYS + ['fence'])
        if stop in ('A', 'B1'):
            return False
        if not qside_loaded:
            load_qside()
        def run(*gens):
            lists = []
            for g_ in gens:
                n_ = 0
                lists.append(g_)
            active = list(lists)
            while active:
                for g_ in list(active):
                    try:
                        next(g_)
                    except StopIteration:
                        active.remove(g_)

        def weighted(gen, w):
            def g():
                done = False
                while not done:
                    for _ in range(w):
                        try:
                            next(gen)
                        except StopIteration:
                            done = True
                            break
                    yield
            return g()

        def nsteps_front(i):
            return 7 + 4 + 8 * len(range(0, 128 * (i + 1), 512))

        def nsteps_back(i):
            return (i // 4 + 1) + 2 * ((i + 2) // 2 + 1)

        def nsteps_scores(i):
            return 8 * len(range(0, 128 * (i + 1), 512))

        run(b2_front(0))
        run(b2_front(1), b2_scores(0))
        run(b2_front(2), b2_scores(1), b2_topk(0))
        for i in range(NT):
            stages = []
            if i + 1 < NT:
                stages.append((b2_topk(i + 1), NBIS if 128 * (i + 2) > TOPK else 1))
            stages.append((b2_back(i), nsteps_back(i)))
            if i + 2 < NT:
                stages.append((b2_scores(i + 2), nsteps_scores(i + 2)))
            if i + 3 < NT:
                stages.append((b2_front(i + 3), 11 * 2.0))
            if i >= 1:
                stages.append((phaseC_tile(i - 1, l == nlayers - 1), 3.0 * R_FRONT))
            R_ = max([n for _, n in stages if isinstance(n, int)] + [1])
            live = [[g_, n, 0] for g_, n in stages]
            r_ = 0
            while live:
                for st in list(live):
                    g_, n, done_ = st
                    want = int(((r_ + 1) * n) // R_) if r_ + 1 < R_ else 10 ** 9
                    while st[2] < want:
                        try:
                            next(g_)
                            st[2] += 1
                        except StopIteration:
                            live.remove(st)
                            break
                r_ += 1
            if i == NT - 3 and l + 1 < nlayers:
                w_n = w_in[l + 1].rearrange("(c p) n -> p c n", p=128)
                op('pool', [lambda e, c=c: e.dma_start(out=wbig[:, c, 0:1552], in_=w_n[:, c, 0:1552])
                            for c in range(8)], writes=['wbig'], dma='wbig')
                gla_prefetched.add(l + 1)
        run(phaseC_tile(NT - 1, l == nlayers - 1))
        phaseC_b(NT - 1, l == nlayers - 1)
        return True

    if stop != 'p0':
        for l in range(nlayers):
            if not layer(l):
                break

    final = list(out_tokens)
    for b, tok in S.last_w.items():
        if tok[0][0] == 'dma':
            final.append(tok)
    S.wait_tokens('sp', final)
    for e in ('pe', 'act', 'dve', 'pool'):
        n = S.cnt[e]
        if n:
            S._wait('sp', (('eng', e), n))
    S.emit()
    es.close()
    return nc


def make_consts():
    c = {}
    c["c_ident"] = np.eye(128, dtype=np.float32)
    s = np.arange(128)[:, None]
    t = np.arange(128)[None, :]
    same = (s // 64) == (t // 64)
    c["c_tri_incl"] = ((s <= t) & same).astype(np.float32)
    c["c_tri_strict"] = ((s > t) & same).astype(np.float32)
    s64 = np.arange(128)[:, None] % 64
    t64 = np.arange(64)[None, :]
    c["c_maskA"] = (s64 <= t64).astype(np.float32)
    tt = np.arange(128)[:, None]
    ss = np.arange(128)[None, :]
    c["c_maskadd"] = np.where(ss <= tt, 0.0, NEG).astype(np.float32)
    c["c_tri01"] = (ss <= tt).astype(np.float32)
    pm = np.zeros((64, 64), np.float32)
    for m in range(8):
        pm[m + 8, m] = 1.0
        pm[m, m + 8] = 1.0
    c["c_pm"] = pm
    inv = (np.float32(500000.0) ** (-np.arange(0, 16, 2, dtype=np.float32) / np.float32(16))).astype(np.float32)
    pos = (np.arange(T, dtype=np.float32) - np.float32(NPAD)).astype(np.float32)
    ang = (pos[:, None] * inv[None, :]).astype(np.float32)
    cos, sin = np.cos(ang).astype(np.float32), np.sin(ang).astype(np.float32)
    ct = np.ones((64, T), np.float32)
    st = np.zeros((64, T), np.float32)
    ct[0:8] = cos.T
    ct[8:16] = cos.T
    st[0:8] = -sin.T
    st[8:16] = sin.T
    c["c_pow2"] = np.tile((0.5 ** np.arange(NBIS + 1, dtype=np.float64)).astype(np.float32)[None, :], (128, 1))
    c["c_ct"] = ct
    c["c_cos_tok"] = np.ascontiguousarray(cos)
    c["c_sin_tok"] = np.ascontiguousarray(sin)
    c["c_st"] = st
    return c


_NC_CACHE = {}


def _in_maps(inputs):
    f = lambda a: np.ascontiguousarray(np.asarray(a, dtype=np.float32))
    consts = make_consts()
    shared = {
        "meta": f(inputs["meta_tokens"]),
        "ln_in_g": f(inputs["ln_in_g"]).reshape(1, D),
        "ln_in_b": f(inputs["ln_in_b"]).reshape(1, D),
        "w_in": f(inputs["w_in"]),
        "gla_wg2": f(inputs["gla_wg2"]),
        "gla_bg": f(inputs["gla_bg"]).reshape(DEPTH, 1, 256),
        "gla_norm_g": f(inputs["gla_norm_g"]).reshape(DEPTH, 1, 128),
        "idx_k_g": f(inputs["idx_k_g"]).reshape(DEPTH, 1, 64),
        "idx_k_b": f(inputs["idx_k_b"]).reshape(DEPTH, 1, 64),
        "w_out": f(inputs["w_out"]),
        "ln_g": f(inputs["ln_g"]).reshape(DEPTH, 1, D),
        "ln_b": f(inputs["ln_b"]).reshape(DEPTH, 1, D),
    }
    shared.update(consts)
    xs = f(inputs["x"])
    return [dict(shared, x=xs[b]) for b in range(xs.shape[0])]


def kernel(**inputs):
    if "nc" not in _NC_CACHE:
        _NC_CACHE["nc"] = build()
    nc = _NC_CACHE["nc"]
    maps = _in_maps(inputs)
    res = run_bass_kernel_spmd(nc, maps, core_ids=list(range(8)))
    return np.stack([np.asarray(r["out"], dtype=np.float32) for r in res.results], axis=0)
```

```python
import math
from contextlib import ExitStack

import numpy as np
import concourse.bass as bass
import concourse.mybir as mybir
from concourse.bass_utils import run_bass_kernel_spmd

F32 = mybir.dt.float32
BF16 = mybir.dt.bfloat16
AF = mybir.ActivationFunctionType
ALU = mybir.AluOpType
AX = mybir.AxisListType

T = 2176
NT = 17
NPAD = 112
D = 1024
DEPTH = 2
IN_W = 3416
C_GQ, C_GK, C_GV, C_LR, C_GZ, C_AQ, C_AK, C_AV, C_IQ, C_IK, C_IW, C_AZ = (
    0, 256, 512, 1024, 1040, 1552, 2064, 2192, 2320, 2832, 2896, 2904)
ALPHA = (2.0 * DEPTH) ** 0.25
LN_EPS = 1e-5
NEG = -1.0e30
NEG2 = -3.0e38
TOPK = 256
NBIS = 24
R_FRONT = 4


class Sched:
    EPOCH = 30000

    def __init__(self, nc, es):
        self.nc, self.es = nc, es
        self.engs = {'pe': nc.tensor, 'act': nc.scalar, 'dve': nc.vector, 'pool': nc.gpsimd, 'sp': nc.sync}
        self.prog = {e: [] for e in self.engs}
        self.cnt = {e: 0 for e in self.engs}
        self.sems = {}
        self.semval = {}
        self.waited = {e: {} for e in self.engs}
        self.last_w = {}
        self.readers = {}
        self.pe_rt = {}

    def sem(self, key):
        if key not in self.sems:
            self.sems[key] = self.es.enter_context(self.nc.semaphore("s%d" % len(self.sems)))
        return self.sems[key]

    def _wait(self, e, tok):
        k, v = tok
        if k[0] == 'dma':
            v = self.semval[k]
        if self.waited[e].get(k, 0) >= v:
            return
        if e == 'pe' and k[0] == 'eng' and k[1] == 'pe':
            return
        self.waited[e][k] = v
        self.prog[e].append(('wait', k, v))

    def op(self, e, fn, reads=(), writes=(), dma=None, rt='full'):
        deps = []
        if e == 'pe':
            for b in writes:
                last = self.pe_rt.get(b)
                if last is not None and last[0] != rt:
                    k, v = last[1]
                    if self.waited[e].get(k, 0) < v:
                        self.waited[e][k] = v
                        self.prog[e].append(('wait', k, v))
        for b in reads:
            if b in self.last_w:
                deps.append(self.last_w[b])
        for b in writes:
            if b in self.last_w:
                deps.append(self.last_w[b])
            deps.extend(self.readers.get(b, {}).items())
        for tok in deps:
            self._wait(e, tok)
        fns = fn if isinstance(fn, (list, tuple)) else [fn]
        if dma is None:
            assert len(fns) == 1
            self.cnt[e] += 1
            n = self.cnt[e]
            k = ('eng', e)
            v = n
            inc = 1
        else:
            k = ('dma', dma)
            self.semval[k] = self.semval.get(k, 0) + 16 * len(fns)
            v = self.semval[k]
            inc = 16
        self.sem(k)
        for f in fns:
            self.prog[e].append(('op', f, k, inc, v))
        tok = (k, v)
        if e == 'pe':
            for b in writes:
                self.pe_rt[b] = (rt, tok)
        for b in writes:
            self.last_w[b] = tok
            self.readers[b] = {}
        for b in reads:
            r = self.readers.setdefault(b, {})
            r[k] = max(r.get(k, 0), v)
        return tok

    def wait_tokens(self, e, toks):
        for t in toks:
            self._wait(e, t)

    def emit(self):
        needed = set()
        for e in self.prog:
            for it in self.prog[e]:
                if it[0] == 'wait' and it[1][0] == 'eng':
                    needed.add((it[1], it[2]))
        rank = {}
        for e in self.prog:
            r = 0
            for it in self.prog[e]:
                if it[0] == 'op' and it[2][0] == 'eng':
                    tok = (it[2], it[4])
                    if tok in needed:
                        r += 1
                        rank[tok] = r
        self.n_inc = {e: 0 for e in self.prog}
        with self.nc.Block() as block:
            def mk(e):
                def f(eng):
                    for it in self.prog[e]:
                        if it[0] == 'wait':
                            k, v = it[1], it[2]
                            if k[0] == 'eng':
                                v = rank[(k, v)]
                            eng.wait_ge(self.sems[k], v)
                        else:
                            ins = it[1](eng)
                            k = it[2]
                            if k[0] == 'dma':
                                ins.then_inc(self.sems[k], 16)
                            elif (k, it[4]) in needed:
                                ins.then_inc(self.sems[k], 1)
                                self.n_inc[e] += 1
                return f
            block.tensor(mk('pe'))
            block.scalar(mk('act'))
            block.vector(mk('dve'))
            block.gpsimd(mk('pool'))
            block.sync(mk('sp'))


def build(nlayers=DEPTH, stop=None, debug=False):
    nc = bass.Bass("TRN2", target_bir_lowering=False)
    es = ExitStack()
    S = Sched(nc, es)
    op = S.op

    def dram(name, shape, dt=F32, kind="ExternalInput"):
        return nc.dram_tensor(name, list(shape), dt, kind=kind).ap()

    scratch_kind = "ExternalOutput" if debug else "Internal"
    x = dram("x", [2048, D])
    meta = dram("meta", [16, D])
    ln_in_g = dram("ln_in_g", [1, D])
    ln_in_b = dram("ln_in_b", [1, D])
    w_in = dram("w_in", [DEPTH, D, IN_W])
    gla_wg2 = dram("gla_wg2", [DEPTH, 16, 256])
    gla_bg = dram("gla_bg", [DEPTH, 1, 256])
    gla_norm_g = dram("gla_norm_g", [DEPTH, 1, 128])
    idx_k_g = dram("idx_k_g", [DEPTH, 1, 64])
    idx_k_b = dram("idx_k_b", [DEPTH, 1, 64])
    w_out = dram("w_out", [DEPTH, D, D])
    ln_g = dram("ln_g", [DEPTH, 1, D])
    ln_b = dram("ln_b", [DEPTH, 1, D])
    c_ident = dram("c_ident", [128, 128])
    c_tri_incl = dram("c_tri_incl", [128, 128])
    c_tri_strict = dram("c_tri_strict", [128, 128])
    c_maskA = dram("c_maskA", [128, 64])
    c_maskadd = dram("c_maskadd", [128, 128])
    c_tri01 = dram("c_tri01", [128, 128])
    c_pm = dram("c_pm", [64, 64])
    c_ct = dram("c_ct", [64, T])
    c_st = dram("c_st", [64, T])
    c_pow2 = dram("c_pow2", [128, NBIS + 1])
    c_cos_tok = dram("c_cos_tok", [T, 8])
    c_sin_tok = dram("c_sin_tok", [T, 8])
    out = dram("out", [2048, D], kind="ExternalOutput")
    hd = dram("hd", [T, D], kind=scratch_kind)
    ogd = dram("ogd", [NT, 128, 512], BF16, kind=scratch_kind)
    odd = dram("odd", [NT, 8, 64, 128], BF16, kind=scratch_kind)

    def sb(name, shape, dt=F32):
        return es.enter_context(nc.sbuf_tensor(name, list(shape), dt))

    hT = sb("hT", [128, 8, T], BF16)
    wbig = sb("wbig", [128, 8, 1552], BF16)
    wks = sb("wks", [128, 8, 320], BF16)
    wout = sb("wout", [128, 8, D], BF16)
    wg2 = sb("wg2", [16, 256])
    bg = sb("bg", [1, 256])
    gain_bc = sb("gain_bc", [128, 128])
    ikg_bc = sb("ikg_bc", [128, 64])
    ikb_bc = sb("ikb_bc", [128, 64])
    lng = sb("lng", [128, D])
    lnb = sb("lnb", [128, D])
    ident_f = sb("ident_f", [128, 128])
    ident_b = sb("ident_b", [128, 128], BF16)
    tri_incl = sb("tri_incl", [128, 128])
    tri_strict = sb("tri_strict", [128, 128])
    maskA = sb("maskA", [128, 64])
    maskadd = sb("maskadd", [128, 128])
    pm_b = sb("pm_b", [64, 64], BF16)
    ones_f = sb("ones_f", [128, 128])
    xt = [sb("xt%d" % i, [128, D]) for i in range(2)]
    hn = [sb("hn%d" % i, [128, D]) for i in range(2)]
    stats = sb("stats", [128, 2, 6])
    mv = sb("mv", [128, 2])
    rstd = sb("rstd", [128, 1])
    Fb = [sb("F%d" % i, [128, 512]) for i in range(6)]
    rec = Fb[4]
    Hb = [sb("H%d" % i, [128, 512], BF16) for i in range(6)]
    S_f = sb("S_f", [128, 256])
    S_b = sb("S_b", [128, 256], BF16)
    lrT = sb("lrT", [16, 128])
    ssq = sb("ssq", [128, 4])
    Vaug = sb("Vaug", [128, NT, 2, 65], BF16)
    akR = sb("akR", [64, 2, T], BF16)
    ikR = sb("ikR", [64, T], BF16)
    ctb = sb("ctb", [64, 512])
    stb = sb("stb", [64, 512])
    acc2 = [sb("acc%d" % i, [128, T]) for i in range(2)]
    sel2 = [sb("sel%d" % i, [128, T], BF16) for i in range(2)]
    selT = sb("selT", [128, T], BF16)
    xq2 = [sb("xq%d" % i, [64, 8, 128], BF16) for i in range(4)]
    xi2 = [sb("xi%d" % i, [64, 8, 128], BF16) for i in range(2)]
    szT2 = [sb("szT%d" % i, [64, 8, 128], BF16) for i in range(4)]
    iw2 = [sb("iw%d" % i, [128, 8]) for i in range(2)]
    tiny_t = sb("tiny_t", [128, 1])
    cs_t = sb("cs_t", [128, 16])
    ropet = sb("ropet", [128, 2, 4, 64])
    bs = sb("bs", [128, 16])
    wtab = sb("wtab", [128, NBIS + 1])
    pow2 = sb("pow2", [128, NBIS + 1])
    cbis = sb("cbis", [128, 4])
    odt = sb("odt", [64, 4, 128], BF16)
    odt2 = sb("odt2", [128, 4, 128], BF16)

    ps = [es.enter_context(nc.psum_tensor("ps%d" % i, [128, 512], F32)) for i in range(8)]
    PK = ["ps%d" % i for i in range(8)]

    def ld(dst, src, key, eng='sp'):
        return op(eng, lambda e, d=dst, s=src: e.dma_start(out=d, in_=s), writes=[key], dma=key)

    ld(ident_f[:], c_ident, 'ident_f')
    ld(ident_b[:], c_ident, 'ident_b', 'pool')
    ld(tri_incl[:], c_tri_incl, 'tri_incl')
    ld(tri_strict[:], c_tri_strict, 'tri_strict')
    ld(maskA[:], c_maskA, 'maskA')
    ld(maskadd[:], c_maskadd, 'maskadd')
    ld(pow2[:], c_pow2, 'pow2')
    op('pool', lambda e: e.memset(cbis[:, 0:1], 1.0 / 32), writes=['cbis'])
    op('pool', lambda e: e.memset(cbis[:, 1:2], 33.0 / 64), writes=['cbis'])
    op('pool', lambda e: e.memset(cbis[:, 2:3], TOPK - 0.5), writes=['cbis'])
    op('pool', lambda e: e.memset(cbis[:, 3:4], -1.0e29), writes=['cbis'])
    ld(pm_b[:], c_pm, 'pm_b', 'pool')
    op('pool', lambda e: e.memset(ones_f[:], 1.0), writes=['ones_f'])
    for c in range(8):
        op('pool', lambda e, c=c: e.memset(hT[:, c, 0:NPAD], 0.0), writes=[('hT', 0)])
    op('pool', lambda e: e.memset(Vaug[:].rearrange("p a b c -> p (a b c)"), 1.0), writes=['Vaug'])
    ld(lng[:], ln_in_g.to_broadcast([128, D]), 'lng')
    ld(lnb[:], ln_in_b.to_broadcast([128, D]), 'lnb')

    def layer_norm(src, skey, dst, dkey, geng='pool'):
        op('dve', lambda e: e.bn_stats(out=stats[:, 0, :], in_=src[:, 0:512]), reads=[skey], writes=['stats'])
        op('dve', lambda e: e.bn_stats(out=stats[:, 1, :], in_=src[:, 512:1024]), reads=[skey], writes=['stats'])
        op('dve', lambda e: e.bn_aggr(out=mv[:], in_=stats[:].rearrange("p a b -> p (a b)")),
           reads=['stats'], writes=['mv'])
        op('act', lambda e: e.activation(out=rstd[:], in_=mv[:, 1:2], func=AF.Ln, bias=eps_t[:, 0:1], scale=1.0),
           reads=['mv', 'eps'], writes=['rstd'])
        op('act', lambda e: e.activation(out=rstd[:], in_=rstd[:], func=AF.Exp, scale=-0.5),
           reads=['rstd'], writes=['rstd'])
        op('dve', lambda e: e.tensor_scalar(out=dst[:], in0=src[:], scalar1=mv[:, 0:1], scalar2=rstd[:, 0:1],
                                            op0=ALU.subtract, op1=ALU.mult),
           reads=[skey, 'mv', 'rstd'], writes=[dkey])
        op(geng, lambda e: e.tensor_tensor(out=dst[:], in0=dst[:], in1=lng[:], op=ALU.mult),
           reads=[dkey, 'lng'], writes=[dkey])
        op('pool', lambda e: e.tensor_tensor(out=dst[:], in0=dst[:], in1=lnb[:], op=ALU.add),
           reads=[dkey, 'lnb'], writes=[dkey])

    def build_hT(i, src, skey):
        lo = NPAD if i == 0 else 0
        for half in range(2):
            pk = PK[half]
            for cc in range(4):
                c = half * 4 + cc
                op('pe', lambda e, c=c, cc=cc, half=half: e.transpose(
                    out=ps[half][:, cc * 128:(cc + 1) * 128], in_=src[:, c * 128:(c + 1) * 128], identity=ident_f[:]),
                   reads=[skey, 'ident_f'], writes=[pk])
            op('act', lambda e, half=half: e.activation(
                out=hT[:, half * 4:half * 4 + 4, i * 128 + lo:(i + 1) * 128],
                in_=ps[half][:].rearrange("p (c t) -> p c t", c=4)[:, :, lo:128], func=AF.Copy),
               reads=[], writes=[pk, ('hT', i)])

    eps_t = sb("eps_t", [128, 1])
    op('pool', lambda e: e.memset(eps_t[:], LN_EPS), writes=['eps'])
    op('pool', lambda e: e.memset(tiny_t[:], 1.0e-30), writes=['tiny'])
    nb30k = sb("nb30k", [128, 1])
    op('pool', lambda e: e.memset(nb30k[:], -30000.0), writes=['nb30k'])
    cst = sb("cst", [128, 2])
    op('pool', lambda e: e.memset(cst[:, 0:1], 0.125), writes=['cst'])
    op('pool', lambda e: e.memset(cst[:, 1:2], ALPHA), writes=['cst'])

    import os as _os
    _np0 = int(_os.environ.get('DBG_NP0', NT))
    _dbg = _os.environ.get('DBG_SKIP', '')
    for i in range(_np0):
        j = i % 2
        xk, hk = 'xt%d' % j, 'hn%d' % j
        if i == 0:
            op('pool', lambda e: e.memset(xt[0][:], 0.0), writes=[xk])
            op('sp', lambda e: e.dma_start(out=xt[0][NPAD:128, :], in_=meta), reads=[], writes=[xk], dma=xk)
        else:
            op('sp', lambda e, i=i, j=j: e.dma_start(out=xt[j][:], in_=x[(i - 1) * 128:i * 128, :]),
               writes=[xk], dma=xk)
        if 'ln' not in _dbg:
            layer_norm(xt[j], xk, hn[j], hk, geng='dve')
        op('pool', lambda e, i=i, j=j: e.dma_start(out=hd[i * 128:(i + 1) * 128, :], in_=hn[j][:]),
           reads=[hk], writes=[('hd', i)], dma='hd')
        if 'ht' not in _dbg:
            build_hT(i, hn[j], hk)

    out_tokens = []

    def FK(k):
        return ['F%da' % k, 'F%db' % k]

    def HK(k):
        return ['H%da' % k, 'H%db' % k]

    Fb1 = [acc2[0][:, 512 * k:512 * (k + 1)] for k in range(4)] + [acc2[1][:, 0:512], acc2[1][:, 512:1024]]
    Hb1 = [sel2[0][:, 512 * k:512 * (k + 1)] for k in range(4)] + [sel2[1][:, 0:512], sel2[1][:, 512:1024]]
    Fb2 = [xt[0][:, 0:512], xt[0][:, 512:1024], xt[1][:, 0:512], xt[1][:, 512:1024],
           hn[0][:, 0:512], hn[0][:, 512:1024]]
    hn1b = hn[1][:].bitcast(BF16)
    Hb2 = [hn1b[:, 512 * k:512 * (k + 1)] for k in range(4)] + [odt2[:].rearrange("p c t -> p (c t)"),
                                                              sel2[1][:, 1024:1536]]
    lrT2 = [lrT, sb("lrT_b", [16, 128]), sb("lrT_c", [16, 128])]
    ssq2 = [ssq, sb("ssq_b", [128, 4]), sb("ssq_c", [128, 4])]
    fence_t = sb("fence_t", [128, 2])
    b1f = selT[:].bitcast(F32)
    ALIAS_KEYS = ['%s%d%s' % (c_, k, s) for c_ in 'GJMN' for k in range(6) for s in 'ab']
    OWNER_KEYS = ['acc0', 'acc1', 'sel0', 'sel1', 'xt0', 'xt1', 'hn0', 'hn1', 'odt2']

    _acut = int(_os.environ.get('DBG_ACUT', 99))
    _dmask = int(_os.environ.get('DBG_DMASK', 7))
    _na = int(_os.environ.get('DBG_NA', NT))

    def phaseA_tile(i):
        tc = slice(i * 128, (i + 1) * 128)
        hk = ('hT', i)
        q_ = i % 3
        FB, HB = ((Fb, Hb), (Fb1, Hb1), (Fb2, Hb2))[q_]
        qkT, F1, F2, F3, o_sb, F5 = FB
        v_bf, sz, H2, H3, og, ogT = HB
        fp, hp = (('F', 'H'), ('G', 'J'), ('M', 'N'))[q_]
        lrT_, ssq_ = lrT2[q_], ssq2[q_]
        lk, sk = 'lrT%d' % q_, 'ssq%d' % q_

        def fkk(k):
            return [fp + '%da' % k, fp + '%db' % k]

        def hkk(k):
            return [hp + '%da' % k, hp + '%db' % k]
        for c in range(4):
            for dc in range(8):
                op('pe', lambda e, c=c, dc=dc: e.matmul(
                    ps[0][:, c * 128:(c + 1) * 128], lhsT=wbig[:, dc, c * 128:(c + 1) * 128], rhs=hT[:, dc, tc],
                    start=(dc == 0), stop=(dc == 7)), reads=['wbig', hk], writes=[PK[0]])
        for dc in range(8):
            op('pe', lambda e, dc=dc: e.matmul(
                ps[1][0:16, 0:128], lhsT=wbig[:, dc, C_LR:C_LR + 16], rhs=hT[:, dc, tc],
                start=(dc == 0), stop=(dc == 7)), reads=['wbig', hk], writes=[PK[1]])
        op('act', lambda e: e.activation(out=qkT[:], in_=ps[0][:], func=AF.Copy), writes=[PK[0]] + fkk(0))
        op('act', lambda e: e.activation(out=lrT_[:], in_=ps[1][0:16, 0:128], func=AF.Copy), writes=[PK[1], lk])
        if _acut <= 1:
            return
        yield
        for (pi, n, off) in ((2, 256, C_GK), (3, 512, C_GV), (4, 512, C_GZ)):
            for dc in range(8):
                op('pe', lambda e, pi=pi, n=n, off=off, dc=dc: e.matmul(
                    ps[pi][:, 0:n], lhsT=hT[:, dc, tc], rhs=wbig[:, dc, off:off + n],
                    start=(dc == 0), stop=(dc == 7)), reads=['wbig', hk], writes=[PK[pi]])
        op('pe', lambda e: e.matmul(ps[1][:, 128:384], lhsT=lrT_[:], rhs=wg2[:], start=True, stop=False),
           reads=[lk, 'wg2'], writes=[PK[1]], rt='k32')
        op('pe', lambda e: e.matmul(ps[1][:, 128:384], lhsT=ones_f[0:1, :], rhs=bg[:], start=False, stop=True),
           reads=['ones_f', 'bg'], writes=[PK[1]], rt='k32')
        if _acut <= 2:
            return
        op('act', lambda e: e.activation(out=F1[:, 0:256], in_=ps[1][:, 128:384], func=AF.Exp, scale=-1.0),
           writes=[PK[1], (fp + '1a')])
        op('act', lambda e: e.activation(out=F1[:, 256:512], in_=F1[:, 0:256], func=AF.Ln, bias=ones_f[:, 0:1],
                                         scale=1.0), reads=[(fp + '1a'), 'ones_f'], writes=[(fp + '1b')])
        op('act', lambda e: e.activation(out=F2[:, 0:256], in_=ps[2][:, 0:256], func=AF.Copy),
           writes=[PK[2], (fp + '2a')])
        op('act', lambda e: e.activation(out=v_bf[:], in_=ps[3][:], func=AF.Copy), writes=[PK[3]] + hkk(0))
        op('act', lambda e: e.activation(out=sz[:], in_=ps[4][:], func=AF.Silu), writes=[PK[4]] + hkk(1))
        if _acut <= 3:
            return
        yield
        for fc in range(2):
            op('pe', lambda e, fc=fc: e.matmul(
                ps[5][:, fc * 128:(fc + 1) * 128], lhsT=F1[:, 256 + fc * 128:256 + (fc + 1) * 128],
                rhs=tri_incl[:], start=True, stop=True), reads=[(fp + '1b'), 'tri_incl'], writes=[PK[5]])
        op('pe', lambda e: e.matmul(ps[5][:, 256:512], lhsT=tri_strict[:], rhs=F1[:, 256:512],
                                    start=True, stop=True), reads=[(fp + '1b'), 'tri_strict'], writes=[PK[5]])
        op('act', lambda e: e.activation(out=F3[:, 0:256], in_=ps[5][:, 0:256], func=AF.Exp, scale=-1.0 / 16),
           writes=[PK[5], (fp + '3a')])
        op('act', lambda e: e.activation(out=F3[:, 256:512], in_=ps[5][:, 0:256], func=AF.Exp, scale=1.0 / 16),
           writes=[PK[5], (fp + '3b')])
        op('act', lambda e: e.activation(out=F2[:, 256:512], in_=ps[5][:, 256:512], func=AF.Exp,
                                         scale=-1.0 / 16), writes=[PK[5], (fp + '2b')])
        if _acut <= 4:
            return
        yield
        (op if _dmask & 1 else (lambda *a, **k: None))('dve', lambda e: e.scalar_tensor_tensor(out=H2[:, 0:256], in0=qkT[:, 0:256], scalar=cst[:, 0:1],
                                                   in1=F3[:, 0:256], op0=ALU.mult, op1=ALU.mult),
           reads=[(fp + '0a'), (fp + '3a'), 'cst'], writes=[(hp + '2a')])
        (op if _dmask & 2 else (lambda *a, **k: None))('dve', lambda e: e.tensor_tensor(out=H2[:, 256:512], in0=qkT[:, 256:512], in1=F3[:, 256:512],
                                            op=ALU.mult), reads=[(fp + '0b'), (fp + '3b')], writes=[(hp + '2b')])
        (op if _dmask & 4 else (lambda *a, **k: None))('dve', lambda e: e.tensor_tensor(out=H3[:, 0:256], in0=F2[:, 0:256], in1=F2[:, 256:512],
                                            op=ALU.mult), reads=[(fp + '2a'), (fp + '2b')], writes=[(hp + '3a')])
        if _acut <= 5:
            return
        yield
        for (c, h) in [(c, h) for h in (0, 2, 1, 3) for c in range(2)]:
            if True:
                fc, pb = h // 2, (h % 2) * 64
                c0 = fc * 128 + 64 * c
                op('pe', lambda e, c=c, h=h, pb=pb, c0=c0: e.matmul(
                    ps[6][64 * c:64 * c + 64, h * 64:(h + 1) * 64],
                    lhsT=H2[pb:pb + 64, 256 + c0:256 + c0 + 64], rhs=H2[pb:pb + 64, c0:c0 + 64],
                    start=True, stop=True), reads=[(hp + '2a'), (hp + '2b')], writes=[PK[6]], rt=pb)
        op('dve', lambda e: e.tensor_tensor(
            out=H3[:, 256:512].rearrange("p (h t) -> p h t", h=4),
            in0=ps[6][:, 0:256].rearrange("p (h t) -> p h t", h=4),
            in1=maskA[:].unsqueeze(1).to_broadcast([128, 4, 64]), op=ALU.mult),
           reads=['maskA'], writes=[PK[6], (hp + '3b')])
        if _acut <= 6:
            return
        yield
        for c in range(2):
            r0 = 64 * c

            def o_out(h, r0=r0):
                if h % 2 == 0:
                    return ps[7][r0:r0 + 64, h * 128:(h + 1) * 128], PK[7]
                return ps[6][r0:r0 + 64, (h // 2) * 128:(h // 2 + 1) * 128], PK[6]

            for hp_ in range(2):
                hs = (2 * hp_, 2 * hp_ + 1)
                for h in hs:
                    fc, pb = h // 2, (h % 2) * 64
                    c0 = fc * 128 + 64 * c
                    oap, okey = o_out(h)
                    op('pe', lambda e, oap=oap, pb=pb, c0=c0, fc=fc: e.matmul(
                        oap, lhsT=H2[pb:pb + 64, c0:c0 + 64],
                        rhs=S_b[pb:pb + 64, fc * 128:(fc + 1) * 128], start=True, stop=False),
                       reads=[(hp + '2a'), 'S_b'], writes=[okey], rt=pb)
                for h in hs:
                    oap, okey = o_out(h)
                    op('pe', lambda e, oap=oap, r0=r0, h=h: e.matmul(
                        oap, lhsT=H3[r0:r0 + 64, 256 + h * 64:256 + (h + 1) * 64],
                        rhs=v_bf[r0:r0 + 64, h * 128:(h + 1) * 128], start=False, stop=True),
                       reads=[(hp + '3b')] + hkk(0), writes=[okey], rt=r0)
            for h in range(4):
                fc, pb = h // 2, (h % 2) * 64
                op('pe', lambda e, r0=r0, h=h, pb=pb, fc=fc: e.matmul(
                    ps[6][pb:pb + 64, 256 + fc * 128:256 + (fc + 1) * 128],
                    lhsT=H3[r0:r0 + 64, h * 64:(h + 1) * 64], rhs=v_bf[r0:r0 + 64, h * 128:(h + 1) * 128],
                    start=True, stop=True), reads=[(hp + '3a')] + hkk(0), writes=[PK[6]], rt=r0)
            for fc in range(2):
                op('dve', lambda e, fc=fc, c=c: e.scalar_tensor_tensor(
                    out=S_f[:, fc * 128:(fc + 1) * 128], in0=S_f[:, fc * 128:(fc + 1) * 128],
                    scalar=F3[:, fc * 128 + 64 * c + 63:fc * 128 + 64 * c + 64],
                    in1=ps[6][:, 256 + fc * 128:256 + (fc + 1) * 128], op0=ALU.mult, op1=ALU.add),
                   reads=[(fp + '3a')], writes=[PK[6], 'S_f'])
            op('act', lambda e: e.activation(out=S_b[:], in_=S_f[:], func=AF.Copy), reads=['S_f'], writes=['S_b'])
        if _acut <= 7:
            return
        yield
        o_v = o_sb[:].rearrange("p (a b) -> p a b", a=2)
        op('act', lambda e: e.activation(out=o_v[:, :, 0:128],
                                         in_=ps[7][:].rearrange("p (a b) -> p a b", a=2)[:, :, 0:128], func=AF.Copy),
           writes=[PK[7]] + fkk(4))
        op('act', lambda e: e.activation(out=o_v[:, :, 128:256],
                                         in_=ps[6][:, 0:256].rearrange("p (a b) -> p a b", a=2), func=AF.Copy),
           writes=[PK[6]] + fkk(4))
        op('dve', lambda e: e.tensor_tensor(out=F5[:], in0=o_sb[:], in1=o_sb[:], op=ALU.mult),
           reads=fkk(4), writes=fkk(5))
        op('dve', lambda e: e.tensor_reduce(out=ssq_[:], in_=F5[:].rearrange("p (h d) -> p h d", h=4),
                                            axis=AX.X, op=ALU.add), reads=fkk(5), writes=[sk])
        op('dve', lambda e: e.tensor_scalar(out=ssq_[:], in0=ssq_[:], scalar1=1.0 / 128, scalar2=LN_EPS,
                                            op0=ALU.mult, op1=ALU.add), reads=[sk], writes=[sk])
        op('act', lambda e: e.activation(out=ssq_[:], in_=ssq_[:], func=AF.Ln), reads=[sk], writes=[sk])
        op('act', lambda e: e.activation(out=ssq_[:], in_=ssq_[:], func=AF.Exp, scale=-0.5), reads=[sk], writes=[sk])
        op('dve', lambda e: e.tensor_tensor(
            out=F5[:].rearrange("p (h d) -> p h d", h=4), in0=o_sb[:].rearrange("p (h d) -> p h d", h=4),
            in1=ssq_[:].unsqueeze(2).to_broadcast([128, 4, 128]), op=ALU.mult),
           reads=fkk(4) + [sk], writes=fkk(5))
        op('pool', lambda e: e.tensor_tensor(
            out=F5[:].rearrange("p (h d) -> p h d", h=4), in0=F5[:].rearrange("p (h d) -> p h d", h=4),
            in1=gain_bc[:].unsqueeze(1).to_broadcast([128, 4, 128]), op=ALU.mult),
           reads=fkk(5) + ['gain_bc'], writes=fkk(5))
        op('pool', lambda e: e.tensor_tensor(out=og[:], in0=F5[:], in1=sz[:], op=ALU.mult),
           reads=fkk(5) + hkk(1), writes=hkk(4))
        if _acut <= 8:
            return
        yield
        psb0 = ps[0][:].bitcast(BF16)
        for h in range(4):
            op('pe', lambda e, h=h: e.transpose(out=psb0[:, h * 128:(h + 1) * 128],
                                                in_=og[:, h * 128:(h + 1) * 128], identity=ident_b[:]),
               reads=hkk(4) + ['ident_b'], writes=[PK[0]])
        op('act', lambda e: e.activation(out=ogT[:], in_=psb0[:, 0:512], func=AF.Copy), writes=[PK[0]] + hkk(5))
        op('sp', lambda e: e.dma_start(out=ogd[i], in_=ogT[:]), reads=hkk(5), writes=[('ogd', i)],
           dma='ogd')

    def phaseB1_tile(i):
        tc = slice(i * 128, (i + 1) * 128)
        hk = ('hT', i)
        for (c0, n, off) in ((0, 128, 128), (128, 64, 256)):
            for dc in range(8):
                op('pe', lambda e, c0=c0, n=n, off=off, dc=dc: e.matmul(
                    ps[1][:, c0:c0 + n], lhsT=hT[:, dc, tc], rhs=wks[:, dc, off:off + n],
                    start=(dc == 0), stop=(dc == 7)), reads=['wks', hk], writes=[PK[1]])
        op('act', lambda e: e.activation(out=Vaug[:, i, :, 0:64],
                                         in_=ps[1][:, 0:128].rearrange("p (g d) -> p g d", g=2),
                                         func=AF.Copy), writes=[PK[1], 'Vaug'])
        ikf, ikn = b1f[:, 0:64], b1f[:, 64:128]
        op('act', lambda e: e.activation(out=ikf, in_=ps[1][:, 128:192], func=AF.Copy), writes=[PK[1], 'selT'])
        op('dve', lambda e: e.bn_stats(out=stats[:, 0, :], in_=ikf), reads=['selT'], writes=['stats'])
        op('dve', lambda e: e.bn_aggr(out=mv[:], in_=stats[:, 0, :]), reads=['stats'], writes=['mv'])
        op('act', lambda e: e.activation(out=rstd[:], in_=mv[:, 1:2], func=AF.Ln, bias=eps_t[:, 0:1], scale=1.0),
           reads=['mv', 'eps'], writes=['rstd'])
        op('act', lambda e: e.activation(out=rstd[:], in_=rstd[:], func=AF.Exp, scale=-0.5),
           reads=['rstd'], writes=['rstd'])
        op('dve', lambda e: e.tensor_scalar(out=ikn, in0=ikf, scalar1=mv[:, 0:1], scalar2=rstd[:, 0:1],
                                            op0=ALU.subtract, op1=ALU.mult),
           reads=['selT', 'mv', 'rstd'], writes=['selT'])
        op('pool', lambda e: e.tensor_tensor(out=ikn, in0=ikn, in1=ikg_bc[:], op=ALU.mult),
           reads=['selT', 'ikg_bc'], writes=['selT'])
        op('pool', lambda e: e.tensor_tensor(out=ikn, in0=ikn, in1=ikb_bc[:], op=ALU.add),
           reads=['selT', 'ikb_bc'], writes=['selT'])
        yield
        op('pe', lambda e: e.transpose(out=ps[2][0:64, 0:128], in_=ikn, identity=ident_f[:]),
           reads=['selT', 'ident_f'], writes=[PK[2]])
        op('act', lambda e: e.activation(out=ikR[:, tc], in_=ps[2][0:64, 0:128], func=AF.Copy),
           writes=[PK[2], ('ikR', i // 4)])

    def phaseB1_block(bi):
        b0 = bi * 512
        n = min(512, T - b0)
        bc = slice(b0, b0 + n)
        hks = [('hT', t) for t in range(b0 // 128, (b0 + n) // 128)]
        op('sp', lambda e: e.dma_start(out=ctb[:, 0:n], in_=c_ct[:, bc]), writes=['ctb'], dma='ctb')
        op('sp', lambda e: e.dma_start(out=stb[:, 0:n], in_=c_st[:, bc]), writes=['stb'], dma='stb')
        for g in range(2):
            for dc in range(8):
                op('pe', lambda e, g=g, dc=dc: e.matmul(
                    ps[3 + g][0:64, 0:n], lhsT=wks[:, dc, g * 64:(g + 1) * 64], rhs=hT[:, dc, bc],
                    start=(dc == 0), stop=(dc == 7)), reads=['wks'] + hks, writes=[PK[3 + g]])
            op('act', lambda e, g=g: e.activation(out=akR[:, g, bc], in_=ps[3 + g][0:64, 0:n], func=AF.Copy),
               writes=[PK[3 + g], ('akR', g, bi)])
            yield
        for (X, xk) in ((akR[:, 0, bc], ('akR', 0, bi)), (akR[:, 1, bc], ('akR', 1, bi)),
                        (ikR[:, bc], ('ikR', bi))):
            tmp = b1f[0:64, 128:128 + n]
            op('pe', lambda e, X=X: e.matmul(ps[5][0:64, 0:n], lhsT=pm_b[:], rhs=X, start=True, stop=True),
               reads=[xk, 'pm_b'], writes=[PK[5]], rt=0)
            op('dve', lambda e, tmp=tmp: e.tensor_tensor(out=tmp, in0=ps[5][0:64, 0:n], in1=stb[:, 0:n],
                                                          op=ALU.mult),
               reads=['stb'], writes=[PK[5], 'selT'])
            op('pool', lambda e, X=X: e.tensor_tensor(out=X, in0=X, in1=ctb[:, 0:n], op=ALU.mult),
               reads=[xk, 'ctb'], writes=[xk])
            op('pool', lambda e, X=X, tmp=tmp: e.tensor_tensor(out=X, in0=X, in1=tmp, op=ALU.add),
               reads=[xk, 'selT'], writes=[xk])
            yield

    ct_t, st_t = Fb[2][0:64, 0:128], Fb[2][0:64, 256:384]

    def b2_front(i):
        p = i % 2
        p3 = i % 4
        xq, xi, szT, iw_sb = xq2[p3], xi2[p], szT2[p3], iw2[p]
        tc = slice(i * 128, (i + 1) * 128)
        hk = ('hT', i)
        aq_tok, iq_tok, az_tok = Fb[2], Fb[3], Hb[5]
        op('sp', lambda e: e.dma_start(out=cs_t[:, 0:8], in_=c_cos_tok[tc, :]), writes=['cs_t'], dma='cs_t')
        op('sp', lambda e: e.dma_start(out=cs_t[:, 8:16], in_=c_sin_tok[tc, :]), writes=['cs_t'], dma='cs_t')
        for (off, pi, dst_, dkeys) in ((0, 0, aq_tok, FK(2)), (512, 1, iq_tok, FK(3))):
            for dc in range(8):
                op('pe', lambda e, pi=pi, dc=dc, off=off: e.matmul(
                    ps[pi][:, :], lhsT=hT[:, dc, tc], rhs=wbig[:, dc, off:off + 512],
                    start=(dc == 0), stop=(dc == 7)), reads=['wbig', hk], writes=[PK[pi]])
            op('act', lambda e, pi=pi, dst_=dst_: e.activation(out=dst_[:], in_=ps[pi][:], func=AF.Copy),
               writes=[PK[pi]] + dkeys)
            yield
        for dc in range(8):
            op('pe', lambda e, dc=dc: e.matmul(ps[2][:, :], lhsT=hT[:, dc, tc], rhs=wbig[:, dc, 1024:1536],
                                               start=(dc == 0), stop=(dc == 7)),
               reads=['wbig', hk], writes=[PK[2]])
        op('act', lambda e: e.activation(out=az_tok[:], in_=ps[2][:], func=AF.Silu), writes=[PK[2]] + HK(5))
        for dc in range(8):
            op('pe', lambda e, dc=dc: e.matmul(ps[0][:, 0:8], lhsT=hT[:, dc, tc], rhs=wbig[:, dc, 1536:1544],
                                               start=(dc == 0), stop=(dc == 7)),
               reads=['wbig', hk], writes=[PK[0]])
        op('act', lambda e: e.activation(out=iw_sb[:], in_=ps[0][:, 0:8], func=AF.Copy), writes=[PK[0], ('iw', p)])
        yield
        cosb = cs_t[:, 0:8].unsqueeze(1).to_broadcast([128, 8, 8])
        sinb = cs_t[:, 8:16].unsqueeze(1).to_broadcast([128, 8, 8])
        for xi_, (X, xkeys) in enumerate(((aq_tok, FK(2)), (iq_tok, FK(3)))):
            Xv = X[:].rearrange("p (h d) -> p h d", h=8)
            x1, x2 = Xv[:, :, 0:8], Xv[:, :, 8:16]
            tk = ('ropet', xi_)
            t1, t2, t3, t4 = [ropet[:, xi_, k, :].rearrange("p (h d) -> p h d", h=8) for k in range(4)]
            for (o_, a_, b_) in ((t1, x1, cosb), (t2, x2, sinb), (t3, x2, cosb), (t4, x1, sinb)):
                op('pool', lambda e, o_=o_, a_=a_, b_=b_: e.tensor_tensor(out=o_, in0=a_, in1=b_, op=ALU.mult),
                   reads=xkeys + ['cs_t'], writes=[tk])
            op('pool', lambda e, x1=x1, t1=t1, t2=t2: e.tensor_tensor(out=x1, in0=t1, in1=t2, op=ALU.subtract),
               reads=[tk], writes=xkeys)
            op('pool', lambda e, x2=x2, t3=t3, t4=t4: e.tensor_tensor(out=x2, in0=t3, in1=t4, op=ALU.add),
               reads=[tk], writes=xkeys)
            yield
        psb1 = ps[1][:].bitcast(BF16)
        for (X, xkeys, dst, dk_, pp, isbf) in ((aq_tok, FK(2), xq, 'xq', p3, False), (iq_tok, FK(3), xi, 'xi', p, False),
                                               (az_tok, HK(5), szT, 'szT', p3, True)):
            for half in range(2):
                for hh in range(4):
                    h = half * 4 + hh
                    if isbf:
                        op('pe', lambda e, X=X, h=h, hh=hh: e.transpose(
                            out=psb1[0:64, hh * 128:(hh + 1) * 128], in_=X[:, h * 64:(h + 1) * 64],
                            identity=ident_b[:]), reads=xkeys + ['ident_b'], writes=[PK[1]])
                    else:
                        op('pe', lambda e, X=X, h=h, hh=hh: e.transpose(
                            out=ps[0][0:64, hh * 128:(hh + 1) * 128], in_=X[:, h * 64:(h + 1) * 64],
                            identity=ident_f[:]), reads=xkeys + ['ident_f'], writes=[PK[0]])
                if isbf:
                    op('act', lambda e, dst=dst, half=half: e.activation(
                        out=dst[:, half * 4:half * 4 + 4, :],
                        in_=psb1[0:64, 0:512].rearrange("p (h t) -> p h t", h=4), func=AF.Copy),
                       writes=[PK[1], (dk_, pp, half)])
                else:
                    op('act', lambda e, dst=dst, half=half: e.activation(
                        out=dst[:, half * 4:half * 4 + 4, :],
                        in_=ps[0][0:64, :].rearrange("p (h t) -> p h t", h=4), func=AF.Copy),
                       writes=[PK[0], (dk_, pp, half)])
                yield

    def b2_scores(i):
        p = i % 2
        xi, iw_sb, acc = xi2[p], iw2[p], acc2[p]
        ak_ = 'acc%d' % p
        W = 128 * (i + 1)
        for b5 in range(0, W, 512):
            n = min(512, W - b5)
            ikk = [('ikR', b5 // 512)]
            for j in range(8):
                pi = 3 + (j % 2)
                r = Fb[j % 2]
                rk = FK(j % 2)
                op('pe', lambda e, pi=pi, j=j, b5=b5, n=n: e.matmul(
                    ps[pi][:, 0:n], lhsT=xi[:, j, :], rhs=ikR[:, b5:b5 + n], start=True, stop=True),
                   reads=[('xi', p, j // 4)] + ikk, writes=[PK[pi]], rt=0)
                op('act', lambda e, pi=pi, r=r, n=n: e.activation(out=r[:, 0:n], in_=ps[pi][:, 0:n], func=AF.Relu),
                   writes=[PK[pi]] + rk)
                if j == 0:
                    op('dve', lambda e, r=r, b5=b5, n=n: e.tensor_scalar(
                        out=acc[:, b5:b5 + n], in0=r[:, 0:n], scalar1=iw_sb[:, 0:1], scalar2=None, op0=ALU.mult),
                       reads=rk + [('iw', p)], writes=[ak_])
                else:
                    op('dve', lambda e, r=r, b5=b5, n=n, j=j: e.scalar_tensor_tensor(
                        out=acc[:, b5:b5 + n], in0=r[:, 0:n], scalar=iw_sb[:, j:j + 1], in1=acc[:, b5:b5 + n],
                        op0=ALU.mult, op1=ALU.add), reads=rk + [('iw', p), ak_], writes=[ak_])
                yield

    def b2_topk(i):
        p = i % 2
        acc, sel = acc2[p], sel2[p]
        ak_, sk_ = 'acc%d' % p, 'sel%d' % p
        tc = slice(i * 128, (i + 1) * 128)
        W = 128 * (i + 1)
        if W <= TOPK:
            op('dve', lambda e: e.tensor_tensor(out=acc[:, tc], in0=acc[:, tc], in1=maskadd[:], op=ALU.add),
               reads=[ak_, 'maskadd'], writes=[ak_])
            op('dve', lambda e: e.memset(acc[:, 0:NPAD], NEG), reads=[], writes=[ak_])
            op('dve', lambda e: e.tensor_scalar(out=sel[:, 0:W], in0=acc[:, 0:W], scalar1=cbis[:, 3:4],
                                                scalar2=None, op0=ALU.is_ge), reads=[ak_, 'cbis'], writes=[sk_])
            yield
            return
        op('dve', lambda e: e.tensor_reduce(out=bs[:, 0:1], in_=acc[:, 0:W], axis=AX.X, op=ALU.max),
           reads=[ak_], writes=['bs'])
        op('dve', lambda e: e.tensor_reduce(out=bs[:, 1:2], in_=acc[:, 0:W], axis=AX.X, op=ALU.min),
           reads=[ak_], writes=['bs'])
        op('dve', lambda e: e.tensor_tensor(out=acc[:, tc], in0=acc[:, tc], in1=maskadd[:], op=ALU.add),
           reads=[ak_, 'maskadd'], writes=[ak_])
        op('dve', lambda e: e.memset(acc[:, 0:NPAD], NEG), reads=[], writes=[ak_])
        op('dve', lambda e: e.tensor_tensor(out=bs[:, 2:3], in0=bs[:, 0:1], in1=bs[:, 1:2], op=ALU.subtract),
           reads=['bs'], writes=['bs'])
        op('dve', lambda e: e.tensor_scalar(out=bs[:, 3:5], in0=bs[:, 2:3].to_broadcast([128, 2]),
                                            scalar1=cbis[:, 0:1], scalar2=None, op0=ALU.mult),
           reads=['bs', 'cbis'], writes=['bs'])
        op('dve', lambda e: e.tensor_scalar(out=bs[:, 4:5], in0=bs[:, 2:3], scalar1=cbis[:, 1:2], scalar2=None,
                                            op0=ALU.mult), reads=['bs', 'cbis'], writes=['bs'])
        op('dve', lambda e: e.tensor_tensor(out=bs[:, 5:6], in0=bs[:, 1:2], in1=bs[:, 3:4], op=ALU.subtract),
           reads=['bs'], writes=['bs'])
        op('dve', lambda e: e.tensor_tensor(out=bs[:, 6:7], in0=bs[:, 5:6], in1=bs[:, 4:5], op=ALU.add),
           reads=['bs'], writes=['bs'])
        op('dve', lambda e: e.tensor_scalar(out=wtab[:], in0=pow2[:], scalar1=bs[:, 4:5], scalar2=None,
                                            op0=ALU.mult), reads=['bs', 'pow2'], writes=['wtab'])
        m_ = bs[:, 6:7]
        for k in range(NBIS):
            op('dve', lambda e: e.tensor_scalar(out=sel[:, 0:W], in0=acc[:, 0:W], scalar1=m_, scalar2=None,
                                                op0=ALU.is_ge, op1=ALU.add, accum_out=bs[:, 7:8]),
               reads=[ak_, 'bs'], writes=[sk_, 'bs'])
            op('dve', lambda e, k=k: e.tensor_scalar(out=bs[:, 8:9], in0=bs[:, 7:8], scalar1=cbis[:, 2:3],
                                                     scalar2=wtab[:, k:k + 1], op0=ALU.is_ge, op1=ALU.mult),
               reads=['bs', 'cbis', 'wtab'], writes=['bs'])
            op('dve', lambda e, k=k: e.scalar_tensor_tensor(out=m_, in0=bs[:, 8:9], scalar=wtab[:, k + 1:k + 2],
                                                            in1=m_, op0=ALU.subtract, op1=ALU.add),
               reads=['bs', 'wtab'], writes=['bs'])
            yield
        op('dve', lambda e: e.tensor_tensor(out=bs[:, 9:10], in0=m_, in1=wtab[:, NBIS:NBIS + 1], op=ALU.subtract),
           reads=['bs', 'wtab'], writes=['bs'])
        op('dve', lambda e: e.tensor_scalar(out=sel[:, 0:W], in0=acc[:, 0:W], scalar1=bs[:, 9:10], scalar2=None,
                                            op0=ALU.is_ge), reads=[ak_, 'bs'], writes=[sk_])

    def b2_back(i):
        p = i % 2
        p3 = i % 4
        xq, szT, sel = xq2[p3], szT2[p3], sel2[p]
        sk_ = 'sel%d' % p
        psb5 = ps[5][:].bitcast(BF16)
        for kb0 in range(0, i + 1, 4):
            nb = min(4, i + 1 - kb0)
            for kk in range(nb):
                kb = kb0 + kk
                op('pe', lambda e, kk=kk, kb=kb: e.transpose(
                    out=psb5[:, kk * 128:(kk + 1) * 128], in_=sel[:, kb * 128:(kb + 1) * 128], identity=ident_b[:]),
                   reads=[sk_, 'ident_b'], writes=[PK[5]])
            op('act', lambda e, kb0=kb0, nb=nb: e.activation(
                out=selT[:, kb0 * 128:(kb0 + nb) * 128], in_=psb5[:, 0:nb * 128], func=AF.Identity,
                bias=nb30k[:, 0:1], scale=30000.0),
               reads=['nb30k'], writes=[PK[5], 'selT'])
            yield
        for g in range(2):
            for kb0 in range(0, i + 1, 2):
                kbs = [kb for kb in (kb0, kb0 + 1) if kb <= i]
                for kb in kbs:
                    pi = 6 + (kb % 2)
                    op('pe', lambda e, pi=pi, g=g, kb=kb: e.matmul(
                        ps[pi][:, :], lhsT=akR[:, g, kb * 128:(kb + 1) * 128],
                        rhs=xq[:, 4 * g:4 * g + 4, :].rearrange("p h t -> p (h t)"), start=True, stop=False),
                       reads=[('akR', g, kb // 4), ('xq', p3, g)], writes=[PK[pi]], rt=0)
                for kb in kbs:
                    pi = 6 + (kb % 2)
                    op('pe', lambda e, pi=pi, kb=kb: e.matmul(
                        ps[pi][:, :].rearrange("p (h t) -> p h t", h=4), lhsT=ident_b[:],
                        rhs=selT[:, kb * 128:(kb + 1) * 128].unsqueeze(1).to_broadcast([128, 4, 128]),
                        start=False, stop=True), reads=['selT', 'ident_b'], writes=[PK[pi]])
                for kb in kbs:
                    pi = 6 + (kb % 2)
                    E_ = Hb[kb % 2]
                    ek = HK(kb % 2)
                    op('act', lambda e, pi=pi, E_=E_: e.activation(out=E_[:], in_=ps[pi][:], func=AF.Exp,
                                                                   scale=0.125), writes=[PK[pi]] + ek)
                for kb in kbs:
                    E_ = Hb[kb % 2]
                    ek = HK(kb % 2)
                    op('pe', lambda e, g=g, kb=kb, E_=E_: e.matmul(
                        ps[5][0:65, :], lhsT=Vaug[:, kb, g, :], rhs=E_[:], start=(kb == 0), stop=(kb == i)),
                       reads=['Vaug'] + ek, writes=[PK[5]])
                yield
            op('act', lambda e: e.activation(out=rec[64:65, :], in_=ps[5][64:65, :], func=AF.Ln,
                                             bias=tiny_t[64:65, 0:1], scale=1.0),
               reads=['tiny'], writes=[PK[5], 'rec'])
            op('act', lambda e: e.activation(out=rec[64:65, :], in_=rec[64:65, :], func=AF.Exp, scale=-1.0),
               reads=['rec'], writes=['rec'])
            bc_ = Fb[4][0:64, :]
            on_ = Fb[5][0:64, :]
            op('act', lambda e, on_=on_: e.activation(out=on_, in_=ps[5][0:64, :], func=AF.Copy),
               writes=[PK[5]] + FK(5))
            op('pe', lambda e: e.matmul(ps[2][0:64, :], lhsT=ones_f[64:65, 0:64], rhs=rec[64:65, :],
                                        start=True, stop=True), reads=['rec', 'ones_f'], writes=[PK[2]], rt='k32b')
            op('act', lambda e, bc_=bc_: e.activation(out=bc_, in_=ps[2][0:64, :], func=AF.Copy),
               writes=[PK[2]] + FK(4))
            op('pool', lambda e, bc_=bc_, on_=on_: e.tensor_tensor(out=on_, in0=on_, in1=bc_, op=ALU.mult),
               reads=FK(4) + FK(5), writes=FK(5))
            op('pool', lambda e, g=g, on_=on_: e.tensor_tensor(
                out=odt[:], in0=on_.rearrange("p (h t) -> p h t", h=4), in1=szT[:, 4 * g:4 * g + 4, :],
                op=ALU.mult), reads=FK(5) + [('szT', p3, g)], writes=['odt'])
            op('pool', lambda e, g=g: e.dma_start(
                out=odd[i, 4 * g:4 * g + 4].rearrange("h d t -> d h t"), in_=odt[:]),
               reads=['odt'], writes=[('odd', i, g)], dma='odd')
            yield

    def phaseC_tile(i, last):
        j = i % 2
        xk, hk2 = 'xt%d' % j, 'hn%d' % j
        ogt = Hb[4]
        op('sp', lambda e: e.dma_start(out=ogt[:], in_=ogd[i]), reads=[('ogd', i)], writes=HK(4), dma='H4')
        op('sp', lambda e: e.dma_start(out=odt2[:], in_=odd[i].rearrange("(c r) d t -> (r d) c t", r=2)),
           reads=[('odd', i, 0), ('odd', i, 1)], writes=['odt2'], dma='odt2')
        op('sp', lambda e: e.dma_start(out=xt[j][:], in_=hd[i * 128:(i + 1) * 128, :]),
           reads=[('hd', i)], writes=[xk], dma=xk)
        yield
        for nb in range(2):
            cols = slice(nb * 512, (nb + 1) * 512)
            for c in range(4):
                op('pe', lambda e, nb=nb, c=c, cols=cols: e.matmul(
                    ps[nb][:, :], lhsT=ogt[:, c * 128:(c + 1) * 128], rhs=wout[:, c, cols],
                    start=(c == 0), stop=False), reads=HK(4) + ['wout'], writes=[PK[nb]])
            for c in range(4):
                op('pe', lambda e, nb=nb, c=c, cols=cols: e.matmul(
                    ps[nb][:, :], lhsT=odt2[:, c, :], rhs=wout[:, 4 + c, cols],
                    start=False, stop=(c == 3)), reads=['odt2', 'wout'], writes=[PK[nb]])
            op('dve', lambda e, nb=nb, cols=cols: e.scalar_tensor_tensor(
                out=xt[j][:, cols], in0=xt[j][:, cols], scalar=cst[:, 1:2], in1=ps[nb][:, :],
                op0=ALU.mult, op1=ALU.add), reads=[xk, 'cst'], writes=[PK[nb], xk])
            yield
        if i >= 1:
            phaseC_b(i - 1, last)
        layer_norm(xt[j], xk, hn[j], hk2)

    def phaseC_b(i, last):
        j = i % 2
        hk2 = 'hn%d' % j
        if last:
            if i >= 1:
                out_tokens.append(op('pool', lambda e: e.dma_start(
                    out=out[(i - 1) * 128:i * 128, :], in_=hn[j][:]), reads=[hk2], writes=[('out', i)],
                    dma='out'))
        else:
            op('pool', lambda e: e.dma_start(out=hd[i * 128:(i + 1) * 128, :], in_=hn[j][:]),
               reads=[hk2], writes=[('hd', i)], dma='hd')
            build_hT(i, hn[j], hk2)

    gla_prefetched = set()

    def layer(l):
        w_l = w_in[l].rearrange("(c p) n -> p c n", p=128)
        wo_l = w_out[l].rearrange("(c p) n -> p c n", p=128)
        if l not in gla_prefetched:
            op('pool', [lambda e, c=c: e.dma_start(out=wbig[:, c, 0:1552], in_=w_l[:, c, 0:1552]) for c in range(8)],
               writes=['wbig'], dma='wbig')
        op('pool', [lambda e, c=c: e.dma_start(out=wks[:, c, 0:128], in_=w_l[:, c, C_AK:C_AK + 128]) for c in range(8)]
           + [lambda e, c=c: e.dma_start(out=wks[:, c, 128:256], in_=w_l[:, c, C_AV:C_AV + 128]) for c in range(8)]
           + [lambda e, c=c: e.dma_start(out=wks[:, c, 256:320], in_=w_l[:, c, C_IK:C_IK + 64]) for c in range(8)],
           writes=['wks'], dma='wks')
        op('pool', [lambda e, c=c: e.dma_start(out=wout[:, c, :], in_=wo_l[:, c, :]) for c in range(8)],
           writes=['wout'], dma='wout')
        ld(wg2[:], gla_wg2[l], 'wg2')
        ld(bg[:], gla_bg[l], 'bg')
        ld(gain_bc[:], gla_norm_g[l].to_broadcast([128, 128]), 'gain_bc')
        ld(ikg_bc[:], idx_k_g[l].to_broadcast([128, 64]), 'ikg_bc')
        ld(ikb_bc[:], idx_k_b[l].to_broadcast([128, 64]), 'ikb_bc')
        ld(lng[:], ln_g[l].to_broadcast([128, D]), 'lng')
        ld(lnb[:], ln_b[l].to_broadcast([128, D]), 'lnb')
        op('pool', lambda e: e.memset(S_f[:], 0.0), writes=['S_f'])
        op('pool', lambda e: e.memset(S_b[:], 0.0), writes=['S_b'])
        op('pool', lambda e: e.memset(fence_t[:, 0:1], 0.0), writes=OWNER_KEYS + ALIAS_KEYS + ['fence'])
        gensA = [phaseA_tile(i) for i in range(_na)]
        qside_loaded = []

        def load_qside():
            op('pool', [lambda e, c=c, o=o, s=s, n=n: e.dma_start(out=wbig[:, c, o:o + n], in_=w_l[:, c, s:s + n])
                        for c in range(8) for (o, s, n) in ((0, C_AQ, 512), (512, C_IQ, 512), (1024, C_AZ, 512),
                                                            (1536, C_IW, 8))],
               writes=['wbig'], dma='wbig')

        def b1_lane():
            for i in range(NT):
                yield from phaseB1_tile(i)
                if i % 4 == 3:
                    yield from phaseB1_block(i // 4)
            yield from phaseB1_block(4)
        laneB = b1_lane() if stop != 'A' else iter(())
        activeA, nxt, done_steps = [], 0, {}
        laneB_live = True
        while nxt < len(gensA) or activeA or laneB_live:
            if len(activeA) < 3 and nxt < len(gensA) and (not activeA or done_steps[activeA[-1]] >= 3):
                activeA.append(nxt)
                done_steps[nxt] = 0
                nxt += 1
            for gi in list(activeA):
                try:
                    next(gensA[gi])
                    done_steps[gi] += 1
                except StopIteration:
                    activeA.remove(gi)
                if gi == len(gensA) - 1 and done_steps[gi] == 2 and not qside_loaded and stop not in ('A', 'B1'):
                    load_qside()
                    qside_loaded.append(True)
            if laneB_live:
                try:
                    next(laneB)
                except StopIteration:
                    laneB_live = False
        op('pool', lambda e: e.memset(fence_t[:, 1:2], 0.0), writes=OWNER_KEYS + ALIAS_KEYS + ['fence'])
        if stop in ('A', 'B1'):
            return False
        if not qside_loaded:
            load_qside()
        def run(*gens):
            lists = []
            for g_ in gens:
                n_ = 0
                lists.append(g_)
            active = list(lists)
            while active:
                for g_ in list(active):
                    try:
                        next(g_)
                    except StopIteration:
                        active.remove(g_)

        def weighted(gen, w):
            def g():
                done = False
                while not done:
                    for _ in range(w):
                        try:
                            next(gen)
                        except StopIteration:
                            done = True
                            break
                    yield
            return g()

        def nsteps_front(i):
            return 7 + 4 + 8 * len(range(0, 128 * (i + 1), 512))

        def nsteps_back(i):
            return (i // 4 + 1) + 2 * ((i + 2) // 2 + 1)

        def nsteps_scores(i):
            return 8 * len(range(0, 128 * (i + 1), 512))

        run(b2_front(0))
        run(b2_front(1), b2_scores(0))
        run(b2_front(2), b2_scores(1), b2_topk(0))
        for i in range(NT):
            stages = []
            if i + 1 < NT:
                stages.append((b2_topk(i + 1), NBIS if 128 * (i + 2) > TOPK else 1))
            stages.append((b2_back(i), nsteps_back(i)))
            if i + 2 < NT:
                stages.append((b2_scores(i + 2), nsteps_scores(i + 2)))
            if i + 3 < NT:
                stages.append((b2_front(i + 3), 11))
            if i >= 1:
                stages.append((phaseC_tile(i - 1, l == nlayers - 1), 3.0 * R_FRONT))
            R_ = max([n for _, n in stages if isinstance(n, int)] + [1])
            live = [[g_, n, 0] for g_, n in stages]
            r_ = 0
            while live:
                for st in list(live):
                    g_, n, done_ = st
                    want = int(((r_ + 1) * n) // R_) if r_ + 1 < R_ else 10 ** 9
                    while st[2] < want:
                        try:
                            next(g_)
                            st[2] += 1
                        except StopIteration:
                            live.remove(st)
                            break
                r_ += 1
            if i == NT - 3 and l + 1 < nlayers:
                w_n = w_in[l + 1].rearrange("(c p) n -> p c n", p=128)
                op('pool', [lambda e, c=c: e.dma_start(out=wbig[:, c, 0:1552], in_=w_n[:, c, 0:1552])
                            for c in range(8)], writes=['wbig'], dma='wbig')
                gla_prefetched.add(l + 1)
        run(phaseC_tile(NT - 1, l == nlayers - 1))
        phaseC_b(NT - 1, l == nlayers - 1)
        return True

    if stop != 'p0':
        for l in range(nlayers):
            if not layer(l):
                break

    final = list(out_tokens)
    for b, tok in S.last_w.items():
        if tok[0][0] == 'dma':
            final.append(tok)
    S.wait_tokens('sp', final)
    for e in ('pe', 'act', 'dve', 'pool'):
        n = S.cnt[e]
        if n:
            S._wait('sp', (('eng', e), n))
    S.emit()
    es.close()
    return nc


def make_consts():
    c = {}
    c["c_ident"] = np.eye(128, dtype=np.float32)
    s = np.arange(128)[:, None]
    t = np.arange(128)[None, :]
    same = (s // 64) == (t // 64)
    c["c_tri_incl"] = ((s <= t) & same).astype(np.float32)
    c["c_tri_strict"] = ((s > t) & same).astype(np.float32)
    s64 = np.arange(128)[:, None] % 64
    t64 = np.arange(64)[None, :]
    c["c_maskA"] = (s64 <= t64).astype(np.float32)
    tt = np.arange(128)[:, None]
    ss = np.arange(128)[None, :]
    c["c_maskadd"] = np.where(ss <= tt, 0.0, NEG).astype(np.float32)
    c["c_tri01"] = (ss <= tt).astype(np.float32)
    pm = np.zeros((64, 64), np.float32)
    for m in range(8):
        pm[m + 8, m] = 1.0
        pm[m, m + 8] = 1.0
    c["c_pm"] = pm
    inv = (np.float32(500000.0) ** (-np.arange(0, 16, 2, dtype=np.float32) / np.float32(16))).astype(np.float32)
    pos = (np.arange(T, dtype=np.float32) - np.float32(NPAD)).astype(np.float32)
    ang = (pos[:, None] * inv[None, :]).astype(np.float32)
    cos, sin = np.cos(ang).astype(np.float32), np.sin(ang).astype(np.float32)
    ct = np.ones((64, T), np.float32)
    st = np.zeros((64, T), np.float32)
    ct[0:8] = cos.T
    ct[8:16] = cos.T
    st[0:8] = -sin.T
    st[8:16] = sin.T
    c["c_pow2"] = np.tile((0.5 ** np.arange(NBIS + 1, dtype=np.float64)).astype(np.float32)[None, :], (128, 1))
    c["c_ct"] = ct
    c["c_cos_tok"] = np.ascontiguousarray(cos)
    c["c_sin_tok"] = np.ascontiguousarray(sin)
    c["c_st"] = st
    return c


_NC_CACHE = {}


def _in_maps(inputs):
    f = lambda a: np.ascontiguousarray(np.asarray(a, dtype=np.float32))
    consts = make_consts()
    shared = {
        "meta": f(inputs["meta_tokens"]),
        "ln_in_g": f(inputs["ln_in_g"]).reshape(1, D),
        "ln_in_b": f(inputs["ln_in_b"]).reshape(1, D),
        "w_in": f(inputs["w_in"]),
        "gla_wg2": f(inputs["gla_wg2"]),
        "gla_bg": f(inputs["gla_bg"]).reshape(DEPTH, 1, 256),
        "gla_norm_g": f(inputs["gla_norm_g"]).reshape(DEPTH, 1, 128),
        "idx_k_g": f(inputs["idx_k_g"]).reshape(DEPTH, 1, 64),
        "idx_k_b": f(inputs["idx_k_b"]).reshape(DEPTH, 1, 64),
        "w_out": f(inputs["w_out"]),
        "ln_g": f(inputs["ln_g"]).reshape(DEPTH, 1, D),
        "ln_b": f(inputs["ln_b"]).reshape(DEPTH, 1, D),
    }
    shared.update(consts)
    xs = f(inputs["x"])
    return [dict(shared, x=xs[b]) for b in range(xs.shape[0])]


def kernel(**inputs):
    if "nc" not in _NC_CACHE:
        _NC_CACHE["nc"] = build()
    nc = _NC_CACHE["nc"]
    maps = _in_maps(inputs)
    res = run_bass_kernel_spmd(nc, maps, core_ids=list(range(8)))
    return np.stack([np.asarray(r["out"], dtype=np.float32) for r in res.results], axis=0)
```
